# Optimizing a Trainium2 kernel written in Bass

```python
import jax, jax.numpy as jnp
from jax import lax
import numpy as np

D_MODEL = 1024
BATCH = 8
SEQ = 2048
DEPTH = 4

CTX_LEN = 256
GRID_W = 64
HEAD_DIM = 64
A_HEADS = 4
A_KV_HEADS = 2
A_WINDOW = 128
A_BLOCK = 128
B_HEADS = 4
B_WIN_H = 8
B_WIN_W = 16
C_HEADS = 4
C_Q_RANK = 256
C_KV_RANK = 128
C_NOPE = 64
C_ROPE = 32
C_V = 64
C_BLOCK = 128
D_GROUPS = 4
D_CHUNK = 128
A_W = A_HEADS * HEAD_DIM
A_KV_W = A_KV_HEADS * HEAD_DIM
B_W = B_HEADS * HEAD_DIM
C_W = C_HEADS * C_V
D_WIDTH = D_GROUPS * HEAD_DIM
MIX_W = A_W + B_W + C_W + D_WIDTH
IN_SIZES = (A_KV_W, A_KV_W, B_W, B_W, C_KV_RANK, C_ROPE, A_W, B_W, C_Q_RANK, D_WIDTH, D_WIDTH)
KV_COLS = 2 * A_KV_W + 2 * B_W + C_KV_RANK + C_ROPE
IN_W = KV_COLS + A_W + B_W + C_Q_RANK + 2 * D_WIDTH
FF_HIDDEN = -(-8 * D_MODEL // (3 * 256)) * 256
ROPE_BASE = 10000.0
LN_EPS = 1e-6
NEG_INF = -1e30
DN_ALPHA = (2 * DEPTH) ** 0.25
DN_BETA = (8 * DEPTH) ** -0.25

kernel_name = "hybrid_parallel_group_diffusion_trunk"


def _layer_norm(x, g, b):
    xf = x.astype(jnp.float32)
    xc = xf - jnp.mean(xf, -1, keepdims=True)
    var = jnp.mean(xc * xc, -1, keepdims=True)
    return (xc * lax.rsqrt(var + LN_EPS) * g.astype(jnp.float32) + b.astype(jnp.float32)).astype(x.dtype)


def _rms_norm(x, g):
    xf = x.astype(jnp.float32)
    return (xf * lax.rsqrt(jnp.mean(xf * xf, -1, keepdims=True) + LN_EPS) * g.astype(jnp.float32)).astype(x.dtype)


def _rope_1d(x, pos):
    d = x.shape[-1]
    inv = ROPE_BASE ** (-jnp.arange(0, d, 2, dtype=jnp.float32) / d)
    ang = pos.astype(jnp.float32)[:, None] * inv[None, :]
    cos = jnp.cos(ang)[None, :, None, :]
    sin = jnp.sin(ang)[None, :, None, :]
    x1, x2 = jnp.split(x.astype(jnp.float32), 2, axis=-1)
    return jnp.concatenate([x1 * cos - x2 * sin, x1 * sin + x2 * cos], -1).astype(x.dtype)


def _axial_rope(x, rows, cols):
    half = x.shape[-1] // 2
    return jnp.concatenate([_rope_1d(x[..., :half], rows), _rope_1d(x[..., half:], cols)], -1)


def _heads(t, n, d=HEAD_DIM):
    return t.reshape(t.shape[:-1] + (n, d))


def _dense_attn(q, k, v, scale):
    s = jnp.einsum('bqhd,bkhd->bhqk', q, k).astype(jnp.float32) * scale
    p = jax.nn.softmax(s, axis=-1).astype(v.dtype)
    return jnp.einsum('bhqk,bkhd->bqhd', p, v)


def _mixer_a(q, k, v, kc, vc, qc, sink, rows, cols):
    bsz, seq = q.shape[:2]
    nctx = kc.shape[1]
    grp = A_HEADS // A_KV_HEADS
    nb = seq // A_BLOCK
    nl = 3 * A_BLOCK
    scale = HEAD_DIM ** -0.5
    q = _axial_rope(q, rows, cols)
    k = _axial_rope(k, rows, cols)
    qb = q.reshape(bsz, nb, A_BLOCK, A_KV_HEADS, grp, HEAD_DIM)

    def band(t):
        tp = jnp.pad(t, ((0, 0), (A_BLOCK, A_BLOCK), (0, 0), (0, 0))).reshape(bsz, nb + 2, A_BLOCK, A_KV_HEADS, HEAD_DIM)
        return jnp.concatenate([tp[:, :-2], tp[:, 1:-1], tp[:, 2:]], axis=2)

    kb, vb = band(k), band(v)
    qpos = jnp.arange(seq).reshape(nb, A_BLOCK)
    kpos = (jnp.arange(nb)[:, None] - 1) * A_BLOCK + jnp.arange(nl)[None, :]
    valid = ((jnp.abs(qpos[:, :, None] - kpos[:, None, :]) <= A_WINDOW)
             & (kpos[:, None, :] >= 0) & (kpos[:, None, :] < seq))
    s_loc = jnp.einsum('bnqhgd,bnkhd->bnhgqk', qb, kb).astype(jnp.float32) * scale
    s_loc = jnp.where(valid[None, :, None, None], s_loc, NEG_INF)
    s_ctx = jnp.einsum('bnqhgd,bkhd->bnhgqk', qb, kc).astype(jnp.float32) * scale
    sink_f = sink.astype(jnp.float32).reshape(A_KV_HEADS, grp, 1, 1)
    s_sink = jnp.broadcast_to(sink_f, s_ctx.shape[:-1] + (1,))
    p = jax.nn.softmax(jnp.concatenate([s_loc, s_ctx, s_sink], -1), axis=-1).astype(v.dtype)
    o = (jnp.einsum('bnhgqk,bnkhd->bnqhgd', p[..., :nl], vb)
         + jnp.einsum('bnhgqk,bkhd->bnqhgd', p[..., nl:nl + nctx], vc))
    y = o.reshape(bsz, seq, A_W)
    if qc is None:
        return y, None
    qcg = qc.reshape(bsz, nctx, A_KV_HEADS, grp, HEAD_DIM)
    sc = jnp.einsum('bqhgd,bkhd->bhgqk', qcg, kc).astype(jnp.float32) * scale
    pc = jax.nn.softmax(jnp.concatenate([sc, jnp.broadcast_to(sink_f, sc.shape[:-1] + (1,))], -1), axis=-1)
    yc = jnp.einsum('bhgqk,bkhd->bqhgd', pc[..., :nctx].astype(v.dtype), vc).reshape(bsz, nctx, A_W)
    return y, yc


def _mixer_b(q, k, v, kc, vc, qc, rpb):
    bsz, seq = q.shape[:2]
    n_rows = seq // GRID_W
    wh = min(B_WIN_H, n_rows)
    scale = HEAD_DIM ** -0.5
    qg = q.reshape(bsz, n_rows, GRID_W, B_HEADS, HEAD_DIM)
    kg = k.reshape(bsz, n_rows, GRID_W, B_HEADS, HEAD_DIM)
    vg = v.reshape(bsz, n_rows, GRID_W, B_HEADS, HEAD_DIM)
    r = jnp.arange(n_rows)
    r_start = jnp.clip(r - wh // 2, 0, n_rows - wh)
    row_idx = r_start[:, None] + jnp.arange(wh)[None, :]
    kn = kg[:, row_idx]
    vn = vg[:, row_idx]
    cidx = jnp.arange(GRID_W)
    c_start = jnp.clip(cidx - B_WIN_W // 2, 0, GRID_W - B_WIN_W)
    col_ok = (cidx[None, :] >= c_start[:, None]) & (cidx[None, :] < c_start[:, None] + B_WIN_W)
    dr_i = row_idx - r[:, None] + (B_WIN_H - 1)
    dc_i = jnp.clip(cidx[None, :] - cidx[:, None] + (B_WIN_W - 1), 0, 2 * B_WIN_W - 2)
    bias = rpb.astype(jnp.float32)[:, dr_i[:, None, :, None], dc_i[None, :, None, :]]
    bias = jnp.transpose(bias, (1, 0, 2, 3, 4))
    s = jnp.einsum('brqhd,brjkhd->brhqjk', qg, kn).astype(jnp.float32) * scale + bias[None]
    s = jnp.where(col_ok[:, None, :], s, NEG_INF).reshape(bsz, n_rows, B_HEADS, GRID_W, wh * GRID_W)
    s_ctx = jnp.einsum('brqhd,bkhd->brhqk', qg, kc).astype(jnp.float32) * scale
    p = jax.nn.softmax(jnp.concatenate([s, s_ctx], -1), axis=-1).astype(v.dtype)
    p_loc = p[..., :wh * GRID_W].reshape(bsz, n_rows, B_HEADS, GRID_W, wh, GRID_W)
    o = (jnp.einsum('brhqjk,brjkhd->brqhd', p_loc, vn)
         + jnp.einsum('brhqk,bkhd->brqhd', p[..., wh * GRID_W:], vc))
    y = o.reshape(bsz, seq, B_W)
    if qc is None:
        return y, None
    yc = _dense_attn(qc, kc, vc, scale).reshape(bsz, kc.shape[1], B_W)
    return y, yc


def _mixer_c(cq, ckv, kr, ckv_c, kr_c, cq_c, q_norm, kv_norm, w_uq, w_ukv, rows, cols):
    bsz, seq = cq.shape[:2]
    nctx = ckv_c.shape[1]
    scale = (C_NOPE + C_ROPE) ** -0.5

    def queries(t):
        qh = (_rms_norm(t, q_norm) @ w_uq).reshape(t.shape[:2] + (C_HEADS, C_NOPE + C_ROPE))
        return qh[..., :C_NOPE], qh[..., C_NOPE:]

    def keys_values(t, rk):
        kv = (_rms_norm(t, kv_norm) @ w_ukv).reshape(t.shape[:2] + (C_HEADS, C_NOPE + C_V))
        rk_h = jnp.broadcast_to(rk[:, :, None, :], rk.shape[:2] + (C_HEADS, C_ROPE))
        return jnp.concatenate([kv[..., :C_NOPE], rk_h], -1), kv[..., C_NOPE:]

    k_lat, v_lat = keys_values(ckv, _axial_rope(kr[:, :, None, :], rows, cols)[:, :, 0])
    k_ctx, v_ctx = keys_values(ckv_c, kr_c)
    k_all = jnp.concatenate([k_ctx, k_lat], axis=1)
    v_all = jnp.concatenate([v_ctx, v_lat], axis=1)
    qn, qr = queries(cq)
    q = jnp.concatenate([qn, _axial_rope(qr, rows, cols)], -1)
    nb = seq // C_BLOCK
    qb = jnp.moveaxis(q.reshape(bsz, nb, C_BLOCK, C_HEADS, C_NOPE + C_ROPE), 1, 0)
    o = lax.map(lambda qblk: _dense_attn(qblk, k_all, v_all, scale), qb)
    y = jnp.moveaxis(o, 0, 1).reshape(bsz, seq, C_W)
    if cq_c is None:
        return y, None
    qnc, qrc = queries(cq_c)
    yc = _dense_attn(jnp.concatenate([qnc, qrc], -1), k_ctx, v_ctx, scale).reshape(bsz, nctx, C_W)
    return y, yc


def _mixer_d(du, dv, ln_g, ln_b, ws, bs):
    bsz, length = du.shape[:2]
    nc = length // D_CHUNK
    u = jax.nn.gelu(du)
    v = _layer_norm(jax.nn.gelu(dv), ln_g, ln_b)
    vc = v.reshape(bsz, nc, D_CHUNK, D_GROUPS, HEAD_DIM)
    mixed = jnp.einsum('gts,bcsgd->bctgd', ws, vc) + bs.T[:, :, None]
    return u * mixed.reshape(bsz, length, D_WIDTH)


def _swiglu(h, w_gu, w_down):
    g, u = jnp.split(h @ w_gu, 2, axis=-1)
    return (jax.nn.silu(g) * u) @ w_down


def setup_inputs(seed: int = 0) -> dict:
    key = jax.random.key(seed)
    ks = jax.random.split(key, 24)

    def nrm(k, shape, s):
        return jax.random.normal(k, shape, jnp.float32) * s

    L = DEPTH
    return {
        "x": nrm(ks[0], (BATCH, SEQ, D_MODEL), 1.0),
        "c": nrm(ks[1], (BATCH, D_MODEL), 1.0),
        "ctx": nrm(ks[2], (BATCH, CTX_LEN, D_MODEL), 1.0),
        "c_ctx": nrm(ks[3], (D_MODEL,), 1.0),
        "w_mod": nrm(ks[4], (L, D_MODEL, 6 * D_MODEL), 0.5 * D_MODEL ** -0.5),
        "b_mod": nrm(ks[5], (L, 6 * D_MODEL), 0.02),
        "w_in": nrm(ks[6], (L, D_MODEL, IN_W), D_MODEL ** -0.5),
        "a_sink": nrm(ks[7], (L, A_HEADS), 0.5),
        "b_rpb": nrm(ks[8], (L, B_HEADS, 2 * B_WIN_H - 1, 2 * B_WIN_W - 1), 0.2),
        "c_q_norm": 1.0 + nrm(ks[9], (L, C_Q_RANK), 0.02),
        "c_kv_norm": 1.0 + nrm(ks[10], (L, C_KV_RANK), 0.02),
        "c_w_uq": nrm(ks[11], (L, C_Q_RANK, C_HEADS * (C_NOPE + C_ROPE)), C_Q_RANK ** -0.5),
        "c_w_ukv": nrm(ks[12], (L, C_KV_RANK, C_HEADS * (C_NOPE + C_V)), C_KV_RANK ** -0.5),
        "d_ln_g": 1.0 + nrm(ks[13], (L, D_WIDTH), 0.02),
        "d_ln_b": nrm(ks[14], (L, D_WIDTH), 0.02),
        "d_ws": nrm(ks[15], (L, D_GROUPS, D_CHUNK, D_CHUNK), D_CHUNK ** -0.5),
        "d_bs": 1.0 + nrm(ks[16], (L, D_GROUPS, D_CHUNK), 0.02),
        "w_out": nrm(ks[17], (L, MIX_W, D_MODEL), DN_BETA * MIX_W ** -0.5),
        "ln1_g": 1.0 + nrm(ks[18], (L, D_MODEL), 0.02),
        "ln1_b": nrm(ks[19], (L, D_MODEL), 0.02),
        "w_gu": nrm(ks[20], (L, D_MODEL, 2 * FF_HIDDEN), D_MODEL ** -0.5),
        "w_down": nrm(ks[21], (L, FF_HIDDEN, D_MODEL), DN_BETA * FF_HIDDEN ** -0.5),
        "ln2_g": 1.0 + nrm(ks[22], (L, D_MODEL), 0.02),
        "ln2_b": nrm(ks[23], (L, D_MODEL), 0.02),
    }


def reference(x, c, ctx, c_ctx, w_mod, b_mod, w_in, a_sink, b_rpb, c_q_norm, c_kv_norm, c_w_uq, c_w_ukv,
              d_ln_g, d_ln_b, d_ws, d_bs, w_out, ln1_g, ln1_b, w_gu, w_down, ln2_g, ln2_b):
    seq = x.shape[1]
    t = jnp.arange(seq)
    rows, cols = t // GRID_W, t % GRID_W
    split_at = np.cumsum(IN_SIZES)[:-1].tolist()
    s_c = jax.nn.silu(c)
    s_cc = jax.nn.silu(c_ctx)
    for l in range(DEPTH):
        last = l == DEPTH - 1
        m = jnp.split((s_c @ w_mod[l] + b_mod[l])[:, None, :], 6, axis=-1)
        mc = jnp.split(s_cc @ w_mod[l] + b_mod[l], 6, axis=-1)
        h = x * (1 + m[1]) + m[0]
        hc = ctx * (1 + mc[1]) + mc[0]
        ak, av, bk, bv, cckv, ckr, aq, bq, ccq, du, dv = jnp.split(h @ w_in[l], split_at, axis=-1)
        if last:
            akc, avc, bkc, bvc, cckvc, ckrc = jnp.split(hc @ w_in[l][:, :KV_COLS], split_at[:5], axis=-1)
            aqc = bqc = ccqc = None
        else:
            akc, avc, bkc, bvc, cckvc, ckrc, aqc, bqc, ccqc, duc, dvc = jnp.split(hc @ w_in[l], split_at, axis=-1)
            aqc, bqc = _heads(aqc, A_HEADS), _heads(bqc, B_HEADS)
        ya, yac = _mixer_a(_heads(aq, A_HEADS), _heads(ak, A_KV_HEADS), _heads(av, A_KV_HEADS),
                           _heads(akc, A_KV_HEADS), _heads(avc, A_KV_HEADS), aqc, a_sink[l], rows, cols)
        yb, ybc = _mixer_b(_heads(bq, B_HEADS), _heads(bk, B_HEADS), _heads(bv, B_HEADS),
                           _heads(bkc, B_HEADS), _heads(bvc, B_HEADS), bqc, b_rpb[l])
        yc, ycc = _mixer_c(ccq, cckv, ckr, cckvc, ckrc, ccqc, c_q_norm[l], c_kv_norm[l],
                           c_w_uq[l], c_w_ukv[l], rows, cols)
        yd = _mixer_d(du, dv, d_ln_g[l], d_ln_b[l], d_ws[l], d_bs[l])
        y = jnp.concatenate([ya, yb, yc, yd], axis=-1) @ w_out[l]
        x = _layer_norm(DN_ALPHA * x + m[2] * y, ln1_g[l], ln1_b[l])
        x = _layer_norm(DN_ALPHA * x + m[5] * _swiglu(x * (1 + m[4]) + m[3], w_gu[l], w_down[l]),
                        ln2_g[l], ln2_b[l])
        if not last:
            ydc = _mixer_d(duc, dvc, d_ln_g[l], d_ln_b[l], d_ws[l], d_bs[l])
            y_ctx = jnp.concatenate([yac, ybc, ycc, ydc], axis=-1) @ w_out[l]
            ctx = _layer_norm(DN_ALPHA * ctx + mc[2] * y_ctx, ln1_g[l], ln1_b[l])
            ctx = _layer_norm(DN_ALPHA * ctx + mc[5] * _swiglu(ctx * (1 + mc[4]) + mc[3], w_gu[l], w_down[l]),
                              ln2_g[l], ln2_b[l])
    return x
```

```python
import numpy as np
from contextlib import ExitStack
import concourse.bass as bass
import concourse.mybir as mybir
from concourse.bass_utils import run_bass_kernel_spmd

F32 = mybir.dt.float32
BF16 = mybir.dt.bfloat16
AF = mybir.ActivationFunctionType
ALU = mybir.AluOpType

L = 4
D = 1024
S_LAT = 2048
NCTX = 256
T = 2304
NT = 18
FF = 2816
TCS = [(0, 512), (512, 512), (1024, 512), (1536, 512), (2048, 256)]
ALPHA = 8.0 ** 0.25
EPS = 1e-6
NIN = 2752
PL = 88
NPRM = L * PL + 8
NEG = -30000.0
SC_C = 96.0 ** -0.5

ENGS = ["pe", "act", "dve", "pool", "sp"]
SEM_LIMIT = 30000
N_DMA_SEMS = 24


class Buf:
    __slots__ = ("name", "w", "r")

    def __init__(self, name=""):
        self.name = name
        self.w = None
        self.r = []


class Op:
    __slots__ = ("eng", "fn", "waits", "signal", "idx", "sem", "val", "dma", "dma_prev")

    def __init__(self, eng, fn, dma):
        self.eng = eng
        self.fn = fn
        self.dma = dma
        self.waits = []
        self.signal = False
        self.sem = None
        self.val = None
        self.dma_prev = None


class Sched:
    def __init__(self, nc):
        self.nc = nc
        self.ops = {e: [] for e in ENGS}
        self.seen = {e: {p: -1 for p in ENGS} for e in ENGS}
        self.seen_dma = {e: set() for e in ENGS}
        self.dma_ops = []

    def add(self, eng, fn, reads=(), writes=(), dma=False):
        op = Op(eng, fn, dma)
        op.idx = len(self.ops[eng])
        deps = []
        for b in reads:
            if b.w is not None:
                deps.append(b.w)
        for b in writes:
            if b.w is not None:
                deps.append(b.w)
            deps.extend(b.r)
        for b in reads:
            b.r.append(op)
        for b in writes:
            b.w = op
            b.r = []
        best = {}
        for d in deps:
            if d is op:
                continue
            if d.dma:
                if d in self.seen_dma[eng]:
                    continue
                self.seen_dma[eng].add(d)
                op.waits.append(d)
                continue
            if d.eng == eng and eng == "pe":
                continue
            if d.idx <= self.seen[eng][d.eng]:
                continue
            if d.eng not in best or best[d.eng].idx < d.idx:
                best[d.eng] = d
        for p, d in best.items():
            self.seen[eng][p] = d.idx
            op.waits.append(d)
            d.signal = True
        if dma:
            op.signal = True
            n = len(self.dma_ops)
            if n >= N_DMA_SEMS:
                op.dma_prev = self.dma_ops[n - N_DMA_SEMS]
            self.dma_ops.append(op)
        self.ops[eng].append(op)
        return op

    def marks(self):
        out = []
        for e in ["pe", "act", "dve", "pool"]:
            for op in reversed(self.ops[e]):
                if not op.dma:
                    out.append(op)
                    break
        return out

    def fence(self, marks, engines=("pe", "act", "dve", "pool", "sp")):
        for e in engines:
            op = Op(e, lambda g: g.nop(), False)
            op.idx = len(self.ops[e])
            for d in marks:
                if d.eng == e and e == "pe":
                    continue
                if d.idx <= self.seen[e][d.eng]:
                    continue
                self.seen[e][d.eng] = d.idx
                op.waits.append(d)
                d.signal = True
            self.ops[e].append(op)

    def barrier(self):
        self.fence(self.marks(), engines=("pe", "act", "dve", "pool", "sp"))

    def emit(self, stack):
        nc = self.nc
        sems = {}
        for e in ENGS:
            cnt = 0
            for op in self.ops[e]:
                if op.dma or not op.signal:
                    continue
                key = (e, cnt // SEM_LIMIT)
                if key not in sems:
                    sems[key] = stack.enter_context(nc.semaphore(f"s_{e}_{key[1]}"))
                op.sem = sems[key]
                op.val = cnt % SEM_LIMIT + 1
                cnt += 1
        dsems = [stack.enter_context(nc.semaphore(f"s_dma_{i}")) for i in range(N_DMA_SEMS)]
        for i, op in enumerate(self.dma_ops):
            op.sem = dsems[i % N_DMA_SEMS]
            op.val = 16 * (i // N_DMA_SEMS + 1)
        block = stack.enter_context(nc.Block())

        def run(engobj, ops):
            for op in ops:
                if op.dma_prev is not None:
                    engobj.wait_ge(op.dma_prev.sem, op.dma_prev.val)
                for d in op.waits:
                    engobj.wait_ge(d.sem, d.val)
                ins = op.fn(engobj)
                if op.signal:
                    ins.then_inc(op.sem, 16 if op.dma else 1)

        @block.tensor
        def _(e):
            run(e, self.ops["pe"])

        @block.scalar
        def _(e):
            run(e, self.ops["act"])

        @block.vector
        def _(e):
            run(e, self.ops["dve"])

        @block.gpsimd
        def _(e):
            run(e, self.ops["pool"])

        @block.sync
        def _(e):
            run(e, self.ops["sp"])


def b_class(j):
    if 2 <= j <= 13:
        return 0, list(range(j - 2, j + 3))
    if j == 0:
        return 5, [0, 1, 2, 3]
    if j == 1:
        return 9, [0, 1, 2, 3]
    if j == 14:
        return 13, [12, 13, 14, 15]
    return 17, [12, 13, 14, 15]


def build(nl=L):
    import os
    SKIP = os.environ.get('KSKIP', '')
    nc = bass.Bass("TRN2", target_bir_lowering=False)

    def din(name, shape):
        return nc.dram_tensor(name, shape, F32, kind="ExternalInput").ap()

    x_d = din("x", [T, D])
    cT_d = din("cT", [128, 16])
    wmod_d = din("w_mod", [L, D, 6 * D])
    win_d = din("w_in", [L, D, NIN])
    wuq_d = din("w_uq", [L, 256, 768])
    wukv_d = din("w_ukv", [L, 128, 512])
    wsT_d = din("wsT", [L, 128, 512])
    wout_d = din("w_out", [L, D, D])
    wgu_d = din("w_gu", [L, D, 2 * FF])
    wdn_d = din("w_down", [L, FF, D])
    prm_d = din("prm", [128, NPRM])
    dvec_d = din("dvec", [L, 128, 1024])
    rpbt_d = din("rpbt", [L, 4, 128, 21 * 128])
    rope_d = din("rope", [4, 128, S_LAT])
    cst_d = din("cst", [128, 512])
    out_d = nc.dram_tensor("out", [S_LAT, D], F32, kind="ExternalOutput").ap()

    S = Sched(nc)
    with ExitStack() as st:
        def sbuf(name, shape, dt):
            return st.enter_context(nc.sbuf_tensor("sb_" + name, shape, dt))

        xT = sbuf("xT", [128, 8, T], F32)
        hT = sbuf("hT", [128, 8, T], BF16)
        wts = [sbuf(f"wt{i}", [128, 4096], BF16) for i in range(5)]
        ident = sbuf("ident", [128, 128], F32)
        cbf = sbuf("cbf", [128, 512], BF16)
        prm = sbuf("prm", [128, NPRM], F32)
        modv_all = sbuf("modv", [128, 384], F32)
        Bmods = [Buf("modv0"), Buf("modv1")]
        misc = sbuf("misc", [128, 64], F32)
        sT = sbuf("sT", [128, 16], BF16)
        cT = sbuf("cTs", [128, 16], F32)
        rowsb = sbuf("rowsb", [1, 1024], BF16)
        ARB = 27136
        arena = sbuf("arena", [128, ARB], BF16)
        psq = [st.enter_context(nc.psum_tensor(f"psq{i}", [128, 1024], F32)) for i in range(4)]
        ps = [psq[i // 2][:, (i % 2) * 512:(i % 2) * 512 + 512] for i in range(8)]
        psb = [Buf(f"ps{i}") for i in range(8)]

        onesb = cbf[:, 0:128]
        maskA = cbf[:, 128:384]
        identb = cbf[:, 384:512]

        rot = {"ps": 0, "acc": 0, "wt": 0}

        def nextps():
            i = rot["ps"] % 6
            rot["ps"] += 1
            return ps[i], psb[i]

        def nextps2():
            if rot["ps"] % 2:
                rot["ps"] += 1
            i = rot["ps"] % 6
            rot["ps"] += 2
            return psq[i // 2], psb[i], psb[i + 1]

        def accps():
            i = 6 + rot["acc"] % 2
            rot["acc"] += 1
            return ps[i], psb[i]

        wtb = [Buf(f"wt{i}") for i in range(5)]

        def nextwt():
            i = rot["wt"] % 5
            rot["wt"] += 1
            return wts[i], wtb[i]

        def MM(out, lhsT, rhs, start, stop, R, W):
            S.add("pe", lambda e: e.matmul(out, lhsT=lhsT, rhs=rhs, start=start, stop=stop), reads=R, writes=W)

        def ACT(out, in_, func, R, W, **kw):
            S.add("act", lambda e: e.activation(out=out, in_=in_, func=func, **kw), reads=R, writes=W)

        def TT(eng, out, in0, in1, op, R, W):
            S.add(eng, lambda e: e.tensor_tensor(out=out, in0=in0, in1=in1, op=op), reads=R, writes=W)

        def STT(out, in0, scalar, in1, op0, op1, R, W):
            S.add("dve", lambda e: e.scalar_tensor_tensor(out=out, in0=in0, scalar=scalar, in1=in1, op0=op0, op1=op1),
                  reads=R, writes=W)

        def TS(eng, out, in0, s1, op0, R, W, s2=None, op1=None):
            if op1 is None:
                S.add(eng, lambda e: e.tensor_single_scalar(out=out, in_=in0, scalar=s1, op=op0), reads=R, writes=W)
            else:
                S.add(eng, lambda e: e.tensor_scalar(out=out, in0=in0, scalar1=s1, scalar2=s2, op0=op0, op1=op1), reads=R, writes=W)

        def RECIP(out, in_, R, W):
            S.add("dve", lambda e: e.reciprocal(out=out, in_=in_), reads=R, writes=W)

        def COPY(eng, out, in_, R, W):
            S.add(eng, lambda e: e.tensor_copy(out=out, in_=in_), reads=R, writes=W)

        def MEMSET(eng, ap, val, W):
            S.add(eng, lambda e: e.memset(ap, val), writes=W)

        def DMA(eng, out, in_, R, W):
            S.add(eng, lambda e: e.dma_start(out=out, in_=in_), reads=R, writes=W, dma=True)

        def av(off_b, nbytes, dt):
            a = arena[:, off_b // 2:(off_b + nbytes) // 2]
            if dt == F32:
                return a.bitcast(F32)
            return a

        Bx = [[Buf(f"x{c}_{tc}") for tc in range(5)] for c in range(8)]
        Bh = [[Buf(f"h{c}_{tc}") for tc in range(5)] for c in range(8)]
        Bconst = Buf("const")
        Bprm = Buf("prm")
        Bmisc = Buf("misc")
        BsT = Buf("sT")
        Brows = Buf("rowsb")
        Bident = Buf("ident")

        DMA("sp", ident[:, :], cst_d[:, 0:128], [], [Bident])
        DMA("sp", prm[:, :], prm_d[:, :], [], [Bprm])
        DMA("sp", cT[:, :], cT_d[:, :], [], [BsT])
        DMA("pool", cbf[:, 128:512], cst_d[:, 128:512], [], [Bconst])
        MEMSET("pool", cbf[:, 0:128], 1.0, [Bconst])
        MEMSET("pool", rowsb[:, :], 1.0, [Brows])
        ACT(sT[:, :], cT[:, :], AF.Silu, [BsT], [BsT])

        xin = [av(0, 4096, F32), av(4096, 4096, F32)]
        Bxin = [Buf("xin0"), Buf("xin1")]

        def load_x(mps0, mpb0):
          for t in range(NT):
              xi, bxi = xin[t % 2], Bxin[t % 2]
              DMA("sp", xi, x_d[t * 128:(t + 1) * 128, :], [], [bxi])
              tc = min(t // 4, 4)
              for cg in range(2):
                  p_, pb_ = nextps()
                  for i in range(4):
                      c = cg * 4 + i
                      S.add("pe", (lambda o_, i_: (lambda e: e.transpose(out=o_, in_=i_, identity=ident[:, :])))(
                          p_[:, i * 128:(i + 1) * 128], xi[:, c * 128:(c + 1) * 128]),
                          reads=[bxi, Bident], writes=[pb_])
                  outv = xT[:, cg * 4:cg * 4 + 4, t * 128:(t + 1) * 128]
                  inv = p_[:, :].rearrange("p (a b) -> p a b", b=128)
                  if cg == 0:
                      ACT(outv, inv, AF.Copy, [pb_], [Bx[cg * 4 + i][tc] for i in range(4)])
                  else:
                      COPY("dve", outv, inv, [pb_], [Bx[cg * 4 + i][tc] for i in range(4)])
              if t < 12:
                  mod_piece(0, t, mps0, mpb0)


        def col(base, c, s):
            return base + 2 * c + s

        def load_w(src_ap, kc, ncols):
            wt, wb = nextwt()
            view = wt[:, 0:kc * ncols].rearrange("p (k n) -> p k n", n=ncols)
            DMA("pool", view, src_ap.rearrange("(k p) n -> p k n", p=128), [], [wb])
            return view, wb

        def mod_piece(l, i, mod_ps, mod_pb):
            wv, wb = load_w(wmod_d[l, :, i * 512:(i + 1) * 512], 8, 512)
            for oc in range(4):
                o = (i * 4 + oc) * 2
                for k in range(8):
                    MM(mod_ps[:, o:o + 2], wv[:, k, oc * 128:(oc + 1) * 128], sT[:, 2 * k:2 * k + 2], k == 0, k == 7,
                       [wb, BsT], [mod_pb])

        def mod_finish(l, mod_ps, mod_pb):
            pb = l * PL
            modv = modv_all[:, (l % 2) * 192:(l % 2) * 192 + 192]
            Bmod = Bmods[l % 2]
            TT("dve", modv[:, 0:96].rearrange("p (a s) -> p a s", s=2), mod_ps[:, 0:96].rearrange("p (a s) -> p a s", s=2),
               prm[:, pb:pb + 48].unsqueeze(2).broadcast_to([128, 48, 2]), ALU.add, [mod_pb, Bprm], [Bmod])
            TS("dve", modv[:, 96:112], modv[:, 16:32], 1.0, ALU.add, [Bmod], [Bmod])
            TS("dve", modv[:, 112:128], modv[:, 32:48], 1.0 / ALPHA, ALU.mult, [Bmod], [Bmod])
            TS("dve", modv[:, 128:144], modv[:, 64:80], 1.0, ALU.add, [Bmod], [Bmod])
            TS("dve", modv[:, 144:160], modv[:, 80:96], 1.0 / ALPHA, ALU.mult, [Bmod], [Bmod])
            g1 = prm[:, pb + 48:pb + 56].unsqueeze(2).broadcast_to([128, 8, 2])
            b1 = prm[:, pb + 56:pb + 64].unsqueeze(2).broadcast_to([128, 8, 2])
            v3 = lambda a, b: modv[:, a:b].rearrange("p (c s) -> p c s", s=2)
            TT("dve", v3(160, 176), v3(128, 144), g1, ALU.mult, [Bmod, Bprm], [Bmod])
            TT("dve", v3(176, 192), v3(128, 144), b1, ALU.mult, [Bmod, Bprm], [Bmod])
            TT("dve", v3(176, 192), v3(176, 192), v3(48, 64), ALU.add, [Bmod], [Bmod])
            if l >= 1:
                pp = (l - 1) * PL
                g2 = prm[:, pp + 64:pp + 72].unsqueeze(2).broadcast_to([128, 8, 2])
                b2 = prm[:, pp + 72:pp + 80].unsqueeze(2).broadcast_to([128, 8, 2])
                TT("dve", v3(16, 32), v3(96, 112), g2, ALU.mult, [Bmod, Bprm], [Bmod])
                TT("dve", v3(64, 80), v3(96, 112), b2, ALU.mult, [Bmod, Bprm], [Bmod])
                TT("dve", v3(64, 80), v3(64, 80), v3(0, 16), ALU.add, [Bmod], [Bmod])
            ACT(misc[:, 0:4], prm[:, pb + 80:pb + 84], AF.Exp, [Bprm], [Bmisc])
            MEMSET("dve", rowsb[:, 0:512], 0.0, [Brows])
            for h in range(4):
                hh = h // 2
                other = 1 - hh
                COPY("dve", rowsb[0:1, h * 128 + other * 64:h * 128 + other * 64 + 64],
                     misc[0:1, h:h + 1].broadcast_to([1, 64]), [Bmisc], [Brows])

        def layer(l, ln_marks):
            pb = l * PL
            modv = modv_all[:, (l % 2) * 192:(l % 2) * 192 + 192]
            Bmod = Bmods[l % 2]

            for c in range(8 if l == 0 else 0):
                for tc, (t0, n) in enumerate(TCS):
                    s = 1 if tc == 4 else 0
                    ACT(hT[:, c, t0:t0 + n], xT[:, c, t0:t0 + n], AF.Identity, [Bx[c][tc], Bmod], [Bh[c][tc]],
                        scale=modv[:, col(96, c, s):col(96, c, s) + 1], bias=modv[:, col(0, c, s):col(0, c, s) + 1])

            yT = av(0, 9216, BF16).rearrange("p (c t) -> p c t", t=T)
            By = [[Buf(f"y{c}_{tc}") for tc in range(5)] for c in range(2)]
            pts = [av(9216 + i * 1024, 1024, BF16) for i in range(4)]
            Bpt = [Buf(f"pt{i}") for i in range(4)]
            tmpf = [av(13312 + i * 2048, 2048, F32) for i in range(2)]
            Btmp = [Buf(f"tmpf{i}") for i in range(2)]
            rcs = av(17408, 2048, F32)
            Brc = Buf("rc")
            MS = 20480
            rp = {"pt": 0}

            def nextpt():
                i = rp["pt"] % 4
                rp["pt"] += 1
                return pts[i], Bpt[i]

            def inproj_fm(wv, wb, c0, m, tc):
                t0, n = TCS[tc]
                p_, pb_ = nextps()
                for k in range(8):
                    MM(p_[0:m, 0:n], wv[:, k, c0:c0 + m], hT[:, k, t0:t0 + n], k == 0, k == 7, [wb, Bh[k][tc]], [pb_])
                return p_, pb_

            def v_tm(wv, wb, c0, Vt, Bv):
                for tc, (t0, n) in enumerate(TCS):
                    nt = n // 128
                    p_, pb_ = nextps()
                    for i in range(nt):
                        t = t0 // 128 + i
                        for k in range(8):
                            MM(p_[:, i * 128:(i + 1) * 128], hT[:, k, t * 128:(t + 1) * 128], wv[:, k, c0:c0 + 128],
                               k == 0, k == 7, [wb, Bh[k][tc]], [pb_])
                    pv = p_[:, 0:nt * 128].rearrange("p (a b) -> p a b", b=128)
                    tt = t0 // 128
                    ACT(Vt[:, tt:tt + nt, 0:64], pv[:, :, 0:64], AF.Copy, [pb_], [Bv[tc]])
                    COPY("dve", Vt[:, tt:tt + nt, 128:192], pv[:, :, 64:128], [pb_], [Bv[tc]])

            def normalize(O, Ob, hh, ydst, By_w, n):
                rows = slice(hh * 64, hh * 64 + 64)
                oth = slice((1 - hh) * 64, (1 - hh) * 64 + 64)
                RECIP(rcs[rows, 0:n], O[oth, 0:n], [Ob], [Brc])
                TT("dve", ydst, O[rows, 0:n], rcs[rows, 0:n], ALU.mult, [Ob, Brc], By_w)

            def outproj(mi):
                if 'abcd'[mi] in SKIP:
                    return
                wv, wb = load_w(wout_d[l, mi * 256:(mi + 1) * 256, :], 2, 1024)
                for f in range(8):
                    for tc, (t0, n) in enumerate(TCS):
                        s = 1 if tc == 4 else 0
                        p_, pb_ = nextps()
                        MM(p_[:, 0:n], wv[:, 0, f * 128:(f + 1) * 128], yT[:, 0, t0:t0 + n], True, False, [wb, By[0][tc]], [pb_])
                        MM(p_[:, 0:n], wv[:, 1, f * 128:(f + 1) * 128], yT[:, 1, t0:t0 + n], False, True, [wb, By[1][tc]], [pb_])
                        STT(xT[:, f, t0:t0 + n], p_[:, 0:n], modv[:, col(112, f, s):col(112, f, s) + 1], xT[:, f, t0:t0 + n],
                            ALU.mult, ALU.add, [pb_, Bmod, Bx[f][tc]], [Bx[f][tc]])

            def load_rope(which, tc, cos_t, sin_t, Bct):
                t0, n = TCS[tc]
                DMA("sp", cos_t, rope_d[which, :, t0:t0 + n], [], [Bct])
                DMA("sp", sin_t, rope_d[which + 1, :, t0:t0 + n], [], [Bct])

            for p in range(2):
                if p == 1 and ln_marks is not None:
                    S.fence(ln_marks)
                base = MS + p * 16128
                bkT = av(base, 4608, BF16)
                bqT = av(base + 4608, 4608, BF16)
                Vb = av(base + 9216, 6912, BF16).rearrange("p (t c) -> p t c", c=192)
                Bbk = [Buf(f"bk{tc}") for tc in range(5)]
                Bbq = [Buf(f"bq{tc}") for tc in range(5)]
                Bvb = [Buf(f"vb{tc}") for tc in range(5)]
                wB, wbB = load_w(win_d[l, :, 896 + p * 384:896 + (p + 1) * 384], 8, 384)
                tabs = []
                for hh in range(2):
                    wt, wb = nextwt()
                    DMA("pool", wt[:, 0:21 * 128], rpbt_d[l, 2 * p + hh, :, :], [], [wb])
                    tabs.append((wt, wb))
                MEMSET("pool", Vb[:, :, 64:128], 1.0, Bvb)
                for tc, (t0, n) in enumerate(TCS):
                    p1, pb1 = inproj_fm(wB, wbB, 0, 128, tc)
                    ACT(bkT[:, t0:t0 + n], p1[:, 0:n], AF.Copy, [pb1], [Bbk[tc]])
                    p2, pb2 = inproj_fm(wB, wbB, 128, 128, tc)
                    ACT(bqT[:, t0:t0 + n], p2[:, 0:n], AF.Identity, [pb2], [Bbq[tc]], scale=0.125)
                v_tm(wB, wbB, 256, Vb, Bvb)
                for hh in range(2):
                    rows = slice(hh * 64, hh * 64 + 64)
                    tab, tabb = tabs[hh]
                    def b_scores(j):
                        qtc = j // 4
                        slot0, kbs = b_class(j)
                        pA, pAb = nextps()
                        pB, pBb = nextps()
                        qsl = bqT[rows, j * 128:(j + 1) * 128]
                        MM(pA[:, 0:512], identb, tab[:, slot0 * 128:(slot0 + 4) * 128], True, False, [Bconst, tabb], [pAb])
                        for si, kb in enumerate(kbs):
                            if si < 4:
                                MM(pA[:, si * 128:(si + 1) * 128], bkT[rows, kb * 128:(kb + 1) * 128], qsl, False, si == 3,
                                   [Bbk[kb // 4], Bbq[qtc]], [pAb])
                            else:
                                MM(pB[:, 0:128], bkT[rows, kb * 128:(kb + 1) * 128], qsl, True, False, [Bbk[kb // 4], Bbq[qtc]], [pBb])
                                MM(pB[:, 0:128], identb, tab[:, (slot0 + 4) * 128:(slot0 + 5) * 128], False, True, [Bconst, tabb], [pBb])
                        for ci, kb in enumerate((16, 17)):
                            MM(pB[:, (1 + ci) * 128:(2 + ci) * 128], bkT[rows, kb * 128:(kb + 1) * 128], qsl, True, True,
                               [Bbk[4], Bbq[qtc]], [pBb])
                        return pA, pAb, pB, pBb, kbs

                    def b_rest(st_, O, Ob, jj):
                        pA, pAb, pB, pBb, kbs = st_
                        P1, P1b = nextpt()
                        P2, P2b = nextpt()
                        ACT(P1[:, 0:512], pA[:, 0:512], AF.Exp, [pAb], [P1b])
                        lo = 0 if len(kbs) == 5 else 128
                        ACT(P2[:, lo:384], pB[:, lo:384], AF.Exp, [pBb], [P2b])
                        oc = O[:, jj * 128:(jj + 1) * 128]
                        for si, kb in enumerate(kbs):
                            src, sb_ = (P1[:, si * 128:(si + 1) * 128], P1b) if si < 4 else (P2[:, 0:128], P2b)
                            MM(oc, Vb[:, kb, hh * 64:hh * 64 + 128], src, si == 0, False, [Bvb[kb // 4], sb_], [Ob])
                        for ci, kb in enumerate((16, 17)):
                            MM(oc, Vb[:, kb, hh * 64:hh * 64 + 128], P2[:, (1 + ci) * 128:(2 + ci) * 128], False, ci == 1,
                               [Bvb[4], P2b], [Ob])

                    pend = b_scores(0)
                    for j in range(16):
                        qtc, jj = divmod(j, 4)
                        q0, n = TCS[qtc]
                        if jj == 0:
                            O, Ob = accps()
                        st_ = pend
                        if j + 1 < 16:
                            pend = b_scores(j + 1)
                        b_rest(st_, O, Ob, jj)
                        if jj == 3:
                            normalize(O, Ob, hh, yT[rows, p, q0:q0 + n], [By[p][qtc]], n)
                    q0, n = TCS[4]
                    O, Ob = accps()
                    for ci, kb in enumerate((16, 17)):
                        sp_, spb = nextps()
                        MM(sp_[:, 0:n], bkT[rows, kb * 128:(kb + 1) * 128], bqT[rows, q0:q0 + n], True, True, [Bbk[4], Bbq[4]], [spb])
                        P, Pb = nextpt()
                        ACT(P[:, 0:n], sp_[:, 0:n], AF.Exp, [spb], [Pb])
                        MM(O[:, 0:n], Vb[:, kb, hh * 64:hh * 64 + 128], P[:, 0:n], ci == 0, ci == 1, [Bvb[4], Pb], [Ob])
                    normalize(O, Ob, hh, yT[rows, p, q0:q0 + n], [By[p][4]], n)
            mk = S.marks()
            outproj(1)
            S.fence(mk)

            akT = av(MS, 4608, BF16)
            aqT = av(MS + 4608, 9216, BF16).rearrange("p (c t) -> p c t", t=T)
            Va = av(MS + 13824, 6912, BF16).rearrange("p (t c) -> p t c", c=192)
            ropeA = [[av(MS + 20736 + (2 * i + j) * 2048, 2048, F32) for j in range(2)] for i in range(2)]
            Bak = [Buf(f"ak{tc}") for tc in range(5)]
            Baq = [[Buf(f"aq{c}_{tc}") for tc in range(5)] for c in range(2)]
            Bva = [Buf(f"va{tc}") for tc in range(5)]
            Bra = [Buf("ropeA0"), Buf("ropeA1")]
            wA1, wbA1 = load_w(win_d[l, :, 0:512], 8, 512)
            wA2, wbA2 = load_w(win_d[l, :, 512:896], 8, 384)
            MEMSET("pool", Va[:, :, 64:128], 1.0, Bva)
            for tc, (t0, n) in enumerate(TCS):
                lat = tc < 4
                if lat:
                    cos_t, sin_t = ropeA[tc % 2]
                    load_rope(0, tc, cos_t, sin_t, Bra[tc % 2])
                for (wv, wb, c0, dst, bd) in [(wA1, wbA1, 0, akT[:, t0:t0 + n], Bak[tc]),
                                              (wA1, wbA1, 256, aqT[:, 0, t0:t0 + n], Baq[0][tc]),
                                              (wA2, wbA2, 0, aqT[:, 1, t0:t0 + n], Baq[1][tc])]:
                    p1, pb1 = inproj_fm(wv, wb, c0, 128, tc)
                    if lat:
                        p2, pb2 = inproj_fm(wv, wb, c0 + 128, 128, tc)
                        TT("dve", tmpf[0][:, 0:n], p1[:, 0:n], cos_t[:, 0:n], ALU.mult, [pb1, Bra[tc % 2]], [Btmp[0]])
                        TT("dve", tmpf[1][:, 0:n], p2[:, 0:n], sin_t[:, 0:n], ALU.mult, [pb2, Bra[tc % 2]], [Btmp[1]])
                        TT("pool", dst, tmpf[0][:, 0:n], tmpf[1][:, 0:n], ALU.add, [Btmp[0], Btmp[1]], [bd])
                    else:
                        ACT(dst, p1[:, 0:n], AF.Copy, [pb1], [bd])
            v_tm(wA2, wbA2, 256, Va, Bva)

            def attn_a_chunk(cq, hh, qtc):
                h = 2 * hh + cq
                rows = slice(hh * 64, hh * 64 + 64)
                q0, n = TCS[qtc]
                O, Ob = accps()
                steps = [(kb, 0, n, q0, []) for kb in (16, 17)]
                if qtc < 4:
                    n0 = qtc * 4
                    for kb in range(max(0, n0 - 1), min(15, n0 + 4) + 1):
                        qlo = max(n0, kb - 1)
                        qhi = min(n0 + 3, kb + 1)
                        masks = []
                        for nq in range(qlo, qhi + 1):
                            if nq == kb + 1:
                                masks.append(((nq - n0) * 128, 0))
                            elif nq == kb - 1:
                                masks.append(((nq - n0) * 128, 1))
                        steps.append((kb, (qlo - n0) * 128, (qhi - qlo + 1) * 128, qlo * 128, masks))

                def score(st_):
                    kb, coff, w, qoff, masks = st_
                    sp_, spb = nextps()
                    MM(sp_[:, coff:coff + w], akT[rows, kb * 128:(kb + 1) * 128], aqT[rows, cq, qoff:qoff + w], True,
                       len(masks) == 0, [Bak[min(kb // 4, 4)], Baq[cq][qtc]], [spb])
                    for mi_, (co, m) in enumerate(masks):
                        MM(sp_[:, co:co + 128], identb, maskA[:, m * 128:(m + 1) * 128], False, mi_ == len(masks) - 1,
                           [Bconst], [spb])
                    return sp_, spb

                pend = [score(steps[0])]
                if len(steps) > 1:
                    pend.append(score(steps[1]))
                for si, st_ in enumerate(steps):
                    kb, coff, w, qoff, masks = st_
                    sp_, spb = pend.pop(0)
                    if si + 2 < len(steps):
                        pend.append(score(steps[si + 2]))
                    P, Pb = nextpt()
                    ACT(P[:, coff:coff + w], sp_[:, coff:coff + w], AF.Exp, [spb], [Pb], scale=0.125)
                    MM(O[:, coff:coff + w], Va[:, kb, hh * 64:hh * 64 + 128], P[:, coff:coff + w], si == 0, False,
                       [Bva[min(kb // 4, 4)], Pb], [Ob])
                MM(O[:, 0:n], rowsb[0:1, h * 128:(h + 1) * 128], rowsb[0:1, 512:512 + n], False, True, [Brows], [Ob])
                normalize(O, Ob, hh, yT[rows, cq, q0:q0 + n], [By[cq][qtc]], n)

            for cq in range(2):
                for hh in range(2):
                    for qtc in range(5):
                        attn_a_chunk(cq, hh, qtc)
            mk = S.marks()
            outproj(0)
            S.fence(mk)

            uT = av(MS, 18432, F32).rearrange("p (c t) -> p c t", t=T)
            vd = av(MS + 18432, 9216, BF16).rearrange("p (t c) -> p t c", c=256)
            prb = av(MS + 27648, 4096, F32)
            dtm = [av(MS + 31744 + i * 1024, 1024, F32) for i in range(2)]
            Bu = [[Buf(f"u{c}_{tc}") for tc in range(5)] for c in range(2)]
            Bvd = [Buf(f"vd{tc}") for tc in range(5)]
            Bprb = Buf("prb")
            Bdt = [Buf("dtm0"), Buf("dtm1")]
            Bst = Buf("dstat")
            wD1, wbD1 = load_w(win_d[l, :, 2240:2496], 8, 256)
            wD2, wbD2 = load_w(win_d[l, :, 2496:2752], 8, 256)
            wsv, wbs = load_w(wsT_d[l, :, :], 1, 512)
            wsT = wsv[:, 0, :]
            DMA("sp", prb[:, :], dvec_d[l, :, :], [], [Bprb])
            epsd = prm[:, L * PL + 1:L * PL + 2]
            for c in range(2):
                for tc, (t0, n) in enumerate(TCS):
                    p1, pb1 = inproj_fm(wD1, wbD1, c * 128, 128, tc)
                    ACT(uT[:, c, t0:t0 + n], p1[:, 0:n], AF.Gelu_apprx_tanh, [pb1], [Bu[c][tc]])
            mvall = rcs[:, 0:36]
            rsall = rcs[:, 64:82]
            nball = rcs[:, 96:114]
            Bst2 = [Buf("dstat0"), Buf("dstat1")]

            def dv_gelu(t):
                tc = min(t // 4, 4)
                p_, pb_ = nextps()
                for k in range(8):
                    MM(p_[:, 0:256], hT[:, k, t * 128:(t + 1) * 128], wD2[:, k, 0:256], k == 0, k == 7, [wbD2, Bh[k][tc]], [pb_])
                ACT(dtm[t % 2][:, :], p_[:, 0:256], AF.Gelu_apprx_tanh, [pb_], [Bdt[t % 2]])

            for t in range(NT):
                par = t % 2
                st6 = misc[:, 8 + 8 * par:14 + 8 * par]
                dv_gelu(t)
                S.add("dve", (lambda a_, b_: (lambda e: e.bn_stats(out=a_, in_=b_)))(st6, dtm[par][:, :]), reads=[Bdt[par]], writes=[Bst2[par]])
                S.add("dve", (lambda a_, b_: (lambda e: e.bn_aggr(out=a_, in_=b_)))(mvall[:, 2 * t:2 * t + 2], st6), reads=[Bst2[par]],
                      writes=[Bst2[par], Brc])
            mv3 = mvall.rearrange("p (t two) -> p t two", two=2)
            ACT(rsall, mv3[:, :, 1], AF.Sqrt, [Brc, Bprm], [Brc], scale=1.0, bias=epsd)
            RECIP(rsall, rsall, [Brc], [Brc])
            STT(nball, mv3[:, :, 0], -1.0, rsall, ALU.mult, ALU.mult, [Brc], [Brc])
            for t in range(NT):
                tc = min(t // 4, 4)
                par = t % 2
                dv_gelu(t)
                ACT(dtm[par][:, :], dtm[par][:, :], AF.Identity, [Bdt[par], Brc], [Bdt[par]], scale=rsall[:, t:t + 1], bias=nball[:, t:t + 1])
                TT("dve", dtm[par][:, :], dtm[par][:, :], prb[:, 0:256], ALU.mult, [Bdt[par], Bprb], [Bdt[par]])
                TT("dve", vd[:, t, :], dtm[par][:, :], prb[:, 256:512], ALU.add, [Bdt[par], Bprb], [Bvd[tc]])
            for gp in range(2):
                for tc, (t0, n) in enumerate(TCS):
                    nt = n // 128
                    pa, pab = nextps()
                    pb2, pbb = nextps()
                    for i in range(nt):
                        t = t0 // 128 + i
                        MM(pa[:, i * 128:(i + 1) * 128], vd[:, t, gp * 128:(gp + 1) * 128], wsT[:, (2 * gp) * 128:(2 * gp + 1) * 128],
                           True, True, [Bvd[tc], wbs], [pab])
                        MM(pb2[:, i * 128:(i + 1) * 128], vd[:, t, gp * 128:(gp + 1) * 128],
                           wsT[:, (2 * gp + 1) * 128:(2 * gp + 2) * 128], True, True, [Bvd[tc], wbs], [pbb])
                    for hh, (pp, ppb) in enumerate([(pa, pab), (pb2, pbb)]):
                        g = 2 * gp + hh
                        rows = slice(hh * 64, hh * 64 + 64)
                        bsb = prb[rows, 512 + g * 128:512 + (g + 1) * 128].unsqueeze(1).broadcast_to([64, nt, 128])
                        TT("dve", tmpf[hh][rows, 0:n].rearrange("p (a b) -> p a b", b=128),
                           pp[rows, 0:n].rearrange("p (a b) -> p a b", b=128), bsb, ALU.add, [ppb, Bprb], [Btmp[hh]])
                        TT("dve", yT[rows, gp, t0:t0 + n], tmpf[hh][rows, 0:n], uT[rows, gp, t0:t0 + n], ALU.mult,
                           [Btmp[hh], Bu[gp][tc]], [By[gp][tc]])
            mk = S.marks()
            outproj(3)
            S.fence(mk)

            cqn = av(MS, 9216, BF16).rearrange("p (c t) -> p c t", t=T)
            ckvn = av(MS + 9216, 4608, BF16)
            krT = av(MS + 13824, 4608, BF16)
            Vc = av(MS + 18432, 6912, BF16).rearrange("p (t c) -> p t c", c=192)
            cosC = av(MS + 25344, 2048, F32)
            sinC = av(MS + 27392, 2048, F32)
            Bcq = [Buf(f"cq{tc}") for tc in range(5)]
            Bckv = [Buf(f"ckv{tc}") for tc in range(5)]
            Bkr = [Buf(f"kr{tc}") for tc in range(5)]
            Brc_ = Buf("ropeC")
            wC1, wbC1 = load_w(win_d[l, :, 1664:1984], 8, 320)
            wC2, wbC2 = load_w(win_d[l, :, 1984:2240], 8, 256)
            wuq, wbuq = load_w(wuq_d[l, :, :], 2, 768)
            wukv_, wbukv = load_w(wukv_d[l, :, :], 1, 512)
            wukv = wukv_[:, 0, :]
            R96 = slice(64, 96)
            epsc = prm[:, L * PL + 1:L * PL + 2]
            def upproj(tc):
                t0, n = TCS[tc]
                lat = tc < 4
                for h in range(4):
                    MEMSET("pool", hT[64:128, h, t0:t0 + n], 0.0, [Bh[h][tc]])
                    pk, pkb = nextps()
                    MM(pk[0:64, 0:n], wukv[:, h * 64:(h + 1) * 64], ckvn[:, t0:t0 + n], True, True, [wbukv, Bckv[tc]], [pkb])
                    ACT(hT[0:64, h, t0:t0 + n], pk[0:64, 0:n], AF.Copy, [pkb], [Bh[h][tc]])
                    COPY("pool", hT[R96, h, t0:t0 + n], krT[R96, t0:t0 + n], [Bkr[tc]], [Bh[h][tc]])
                    pq, pqb = nextps()
                    MM(pq[0:96, 0:n], wuq[:, 0, h * 96:(h + 1) * 96], cqn[:, 0, t0:t0 + n], True, False, [wbuq, Bcq[tc]], [pqb])
                    MM(pq[0:96, 0:n], wuq[:, 1, h * 96:(h + 1) * 96], cqn[:, 1, t0:t0 + n], False, True, [wbuq, Bcq[tc]], [pqb])
                    ACT(hT[0:64, 4 + h, t0:t0 + n], pq[0:64, 0:n], AF.Copy, [pqb], [Bh[4 + h][tc]])
                    if lat:
                        pqs, pqsb = nextps()
                        MM(pqs[0:96, 0:n], wuq[:, 0, 384 + h * 96:384 + (h + 1) * 96], cqn[:, 0, t0:t0 + n], True, False,
                           [wbuq, Bcq[tc]], [pqsb])
                        MM(pqs[0:96, 0:n], wuq[:, 1, 384 + h * 96:384 + (h + 1) * 96], cqn[:, 1, t0:t0 + n], False, True,
                           [wbuq, Bcq[tc]], [pqsb])
                        TT("dve", tmpf[0][R96, 0:n], pq[R96, 0:n], cosC[R96, 0:n], ALU.mult, [pqb, Brc_], [Btmp[0]])
                        TT("dve", tmpf[1][R96, 0:n], pqs[R96, 0:n], sinC[R96, 0:n], ALU.mult, [pqsb, Brc_], [Btmp[1]])
                        TT("pool", hT[R96, 4 + h, t0:t0 + n], tmpf[0][R96, 0:n], tmpf[1][R96, 0:n], ALU.add,
                           [Btmp[0], Btmp[1]], [Bh[4 + h][tc]])
                    else:
                        ACT(hT[R96, 4 + h, t0:t0 + n], pq[R96, 0:n], AF.Copy, [pqb], [Bh[4 + h][tc]])

            for tc, (t0, n) in enumerate(TCS):
                lat = tc < 4
                pkv, pkvb = inproj_fm(wC1, wbC1, 0, 128, tc)
                pkr, pkrb = inproj_fm(wC1, wbC1, 128, 96, tc)
                if lat:
                    pks, pksb = inproj_fm(wC1, wbC1, 224, 96, tc)
                    load_rope(2, tc, cosC, sinC, Brc_)
                sq, sqb = nextpt()
                ACT(sq[:, 0:n], pkv[:, 0:n], AF.Square, [pkvb], [sqb])
                s1, s1b = nextps()
                MM(s1[:, 0:n], onesb, sq[:, 0:n], True, True, [Bconst, sqb], [s1b])
                ACT(rcs[:, 0:n], s1[:, 0:n], AF.Sqrt, [s1b, Bprm], [Brc], scale=1.0 / 128.0, bias=epsc)
                RECIP(rcs[:, 0:n], rcs[:, 0:n], [Brc], [Brc])
                STT(ckvn[:, t0:t0 + n], pkv[:, 0:n], prm[:, pb + 86:pb + 87], rcs[:, 0:n], ALU.mult, ALU.mult,
                    [pkvb, Bprm, Brc], [Bckv[tc]])
                if lat:
                    TT("dve", tmpf[0][R96, 0:n], pkr[R96, 0:n], cosC[R96, 0:n], ALU.mult, [pkrb, Brc_], [Btmp[0]])
                    TT("dve", tmpf[1][R96, 0:n], pks[R96, 0:n], sinC[R96, 0:n], ALU.mult, [pksb, Brc_], [Btmp[1]])
                    TT("pool", krT[R96, t0:t0 + n], tmpf[0][R96, 0:n], tmpf[1][R96, 0:n], ALU.add, [Btmp[0], Btmp[1]], [Bkr[tc]])
                else:
                    ACT(krT[R96, t0:t0 + n], pkr[R96, 0:n], AF.Copy, [pkrb], [Bkr[tc]])
                pq0, pq0b = inproj_fm(wC2, wbC2, 0, 128, tc)
                pq1, pq1b = inproj_fm(wC2, wbC2, 128, 128, tc)
                sq0, sq0b = nextpt()
                sq1, sq1b = nextpt()
                ACT(sq0[:, 0:n], pq0[:, 0:n], AF.Square, [pq0b], [sq0b])
                ACT(sq1[:, 0:n], pq1[:, 0:n], AF.Square, [pq1b], [sq1b])
                s2, s2b = nextps()
                MM(s2[:, 0:n], onesb, sq0[:, 0:n], True, False, [Bconst, sq0b], [s2b])
                MM(s2[:, 0:n], onesb, sq1[:, 0:n], False, True, [Bconst, sq1b], [s2b])
                ACT(rcs[:, 0:n], s2[:, 0:n], AF.Sqrt, [s2b, Bprm], [Brc], scale=1.0 / 256.0, bias=epsc)
                RECIP(rcs[:, 0:n], rcs[:, 0:n], [Brc], [Brc])
                STT(cqn[:, 0, t0:t0 + n], pq0[:, 0:n], prm[:, pb + 84:pb + 85], rcs[:, 0:n], ALU.mult, ALU.mult,
                    [pq0b, Bprm, Brc], [Bcq[tc]])
                STT(cqn[:, 1, t0:t0 + n], pq1[:, 0:n], prm[:, pb + 85:pb + 86], rcs[:, 0:n], ALU.mult, ALU.mult,
                    [pq1b, Bprm, Brc], [Bcq[tc]])
                upproj(tc)
            for p in range(2):
                Bvc = [Buf(f"vc{tc}") for tc in range(5)]
                MEMSET("pool", Vc[:, :, 64:128], 1.0, Bvc)
                for tc, (t0, n) in enumerate(TCS):
                    nt = n // 128
                    p_, pb_ = nextps()
                    for i in range(nt):
                        t = t0 // 128 + i
                        MM(p_[:, i * 128:(i + 1) * 128], ckvn[:, t * 128:(t + 1) * 128], wukv[:, 256 + p * 128:256 + (p + 1) * 128],
                           True, True, [wbukv, Bckv[tc]], [pb_])
                    pv = p_[:, 0:nt * 128].rearrange("p (a b) -> p a b", b=128)
                    tt = t0 // 128
                    ACT(Vc[:, tt:tt + nt, 0:64], pv[:, :, 0:64], AF.Copy, [pb_], [Bvc[tc]])
                    COPY("dve", Vc[:, tt:tt + nt, 128:192], pv[:, :, 64:128], [pb_], [Bvc[tc]])
                for hh in range(2):
                    h = 2 * p + hh
                    rows = slice(hh * 64, hh * 64 + 64)
                    for qtc, (q0, n) in enumerate(TCS):
                        O, Ob = accps()
                        kbl = list(range(NT)) if qtc < 4 else [16, 17]

                        def score2(kb0):
                            pq_, b0, b1 = nextps2()
                            for o_, kb, bb in ((0, kb0, b0), (512, kb0 + 1, b1)):
                                MM(pq_[:, o_:o_ + n], hT[:, h, kb * 128:(kb + 1) * 128], hT[:, 4 + h, q0:q0 + n], True, True,
                                   [Bh[h][min(kb // 4, 4)], Bh[4 + h][qtc]], [bb])
                            return pq_, b0, b1

                        npair = len(kbl) // 2
                        pend = score2(kbl[0])
                        for pi_ in range(npair):
                            pq_, b0, b1 = pend
                            if pi_ + 1 < npair:
                                pend = score2(kbl[2 * (pi_ + 1)])
                            if rp["pt"] % 2:
                                rp["pt"] += 1
                            k_ = (rp["pt"] % 4) // 2
                            rp["pt"] += 2
                            Pq = av(9216 + k_ * 2048, 2048, BF16)
                            Pb0, Pb1 = Bpt[2 * k_], Bpt[2 * k_ + 1]
                            if n == 512:
                                ACT(Pq[:, 0:1024], pq_[:, 0:1024], AF.Exp, [b0, b1], [Pb0, Pb1], scale=SC_C)
                            else:
                                ACT(Pq[:, :].rearrange("p (b x) -> p b x", b=2)[:, :, 0:n],
                                    pq_[:, :].rearrange("p (b x) -> p b x", b=2)[:, :, 0:n], AF.Exp, [b0, b1], [Pb0, Pb1], scale=SC_C)
                            for o_, kb, Pb_ in ((0, kbl[2 * pi_], Pb0), (512, kbl[2 * pi_ + 1], Pb1)):
                                MM(O[:, 0:n], Vc[:, kb, hh * 64:hh * 64 + 128], Pq[:, o_:o_ + n], kb == kbl[0], kb == kbl[-1],
                                   [Bvc[min(kb // 4, 4)], Pb_], [Ob])
                        normalize(O, Ob, hh, yT[rows, p, q0:q0 + n], [By[p][qtc]], n)
            outproj(2)
            S.barrier()


            zb = [av(36864 + i * 1024, 1024, BF16) for i in range(2)]
            zq = [av(38912 + i * 1024, 1024, BF16) for i in range(2)]
            mean_ts = [av(40960 + i * 2048, 2048, F32) for i in range(2)]
            rstd_ts = [av(45056 + i * 2048, 2048, F32) for i in range(2)]
            lt = [av(49152 + i * 2048, 2048, F32) for i in range(2)]
            Bzb = [Buf("zb0"), Buf("zb1")]
            Bzq = [Buf("zq0"), Buf("zq1")]
            Bmeans = [Buf("mean0"), Buf("mean1")]
            Brstds = [Buf("rstd0"), Buf("rstd1")]
            Blt = [Buf("lt0"), Buf("lt1")]
            eps_ln = prm[:, L * PL + 0:L * PL + 1]

            def layernorm(gcol, bcol, emit_h, hmod=None, hBmod=None, hg=160, hb=176):
                hmod = modv if hmod is None else hmod
                hBmod = Bmod if hBmod is None else hBmod
                accs = {}

                def stats_a(tc):
                    t0, n = TCS[tc]
                    sps, spsb = accps()
                    qps, qpsb = accps()
                    accs[tc] = (sps, spsb, qps, qpsb)
                    for c in range(8):
                        z = xT[:, c, t0:t0 + n]
                        zhi = z.bitcast(BF16)[:, 1::2]
                        ACT(zq[c % 2][:, 0:n], z, AF.Square, [Bx[c][tc]], [Bzq[c % 2]])
                        MM(sps[:, 0:n], onesb, zhi, c == 0, c == 7, [Bconst, Bx[c][tc]], [spsb])
                        MM(qps[:, 0:n], onesb, zq[c % 2][:, 0:n], c == 0, c == 7, [Bconst, Bzq[c % 2]], [qpsb])

                def stats_b(tc):
                    t0, n = TCS[tc]
                    sps, spsb, qps, qpsb = accs[tc]
                    mean_t, Bmean = mean_ts[tc % 2], Bmeans[tc % 2]
                    rstd_t, Brstd = rstd_ts[tc % 2], Brstds[tc % 2]
                    ACT(mean_t[:, 0:n], sps[:, 0:n], AF.Identity, [spsb], [Bmean], scale=1.0 / D)
                    TT("dve", rstd_t[:, 0:n], mean_t[:, 0:n], mean_t[:, 0:n], ALU.mult, [Bmean], [Brstd])
                    STT(rstd_t[:, 0:n], qps[:, 0:n], 1.0 / D, rstd_t[:, 0:n], ALU.mult, ALU.subtract, [qpsb, Brstd], [Brstd])
                    ACT(rstd_t[:, 0:n], rstd_t[:, 0:n], AF.Sqrt, [Brstd, Bprm], [Brstd], scale=1.0, bias=eps_ln)
                    RECIP(rstd_t[:, 0:n], rstd_t[:, 0:n], [Brstd], [Brstd])

                def normz(tc):
                    t0, n = TCS[tc]
                    s = 1 if tc == 4 else 0
                    mean_t, Bmean = mean_ts[tc % 2], Bmeans[tc % 2]
                    rstd_t, Brstd = rstd_ts[tc % 2], Brstds[tc % 2]
                    for c in range(8):
                        z = xT[:, c, t0:t0 + n]
                        tq = lt[c % 2]
                        TT("dve", tq[:, 0:n], z, mean_t[:, 0:n], ALU.subtract, [Bx[c][tc], Bmean], [Blt[c % 2]])
                        TT("dve", tq[:, 0:n], tq[:, 0:n], rstd_t[:, 0:n], ALU.mult, [Blt[c % 2], Brstd], [Blt[c % 2]])
                        ACT(z, tq[:, 0:n], AF.Identity, [Blt[c % 2], Bprm], [Bx[c][tc]],
                            scale=prm[:, pb + gcol + c:pb + gcol + c + 1], bias=prm[:, pb + bcol + c:pb + bcol + c + 1])
                        if emit_h:
                            ACT(hT[:, c, t0:t0 + n], tq[:, 0:n], AF.Identity, [Blt[c % 2], hBmod], [Bh[c][tc]],
                                scale=hmod[:, col(hg, c, s):col(hg, c, s) + 1], bias=hmod[:, col(hb, c, s):col(hb, c, s) + 1])

                stats_a(0)
                stats_b(0)
                for tc in range(5):
                    if tc + 1 < 5:
                        stats_a(tc + 1)
                    normz(tc)
                    if tc + 1 < 5:
                        stats_b(tc + 1)

            layernorm(48, 56, True)

            hid = [av(i * 18432, 18432, BF16).rearrange("p (j t) -> p j t", t=T) for i in range(2)]
            sgf = lt
            Bsg = Blt
            pieces = [(0, 4), (4, 4), (8, 4), (12, 4), (16, 4), (20, 2)]
            nxt = l + 1 < nl
            if nxt:
                nmps, nmpb = accps()
            for pi, (j0, nj) in enumerate(pieces):
                hd = hid[pi % 2]
                Bhd = [[Buf(f"hid{j}_{tc}") for tc in range(5)] for j in range(nj)]
                halves = []
                for half in range((nj + 1) // 2):
                    jj0 = half * 2
                    njj = min(2, nj - jj0)
                    halves.append((jj0, njj) + load_w(wgu_d[l, :, (j0 + jj0) * 256:(j0 + jj0 + njj) * 256], 8, njj * 256))
                wdv, wdb = load_w(wdn_d[l, j0 * 128:(j0 + nj) * 128, :], nj, 1024)
                for (jj0, njj, wg, wgb) in halves:
                    for jj in range(njj):
                        j = jj0 + jj
                        for tc, (t0, n) in enumerate(TCS):
                            pg, pgb = nextps()
                            pu, pub = nextps()
                            for k in range(8):
                                MM(pg[:, 0:n], wg[:, k, jj * 256:jj * 256 + 128], hT[:, k, t0:t0 + n], k == 0, k == 7,
                                   [wgb, Bh[k][tc]], [pgb])
                            for k in range(8):
                                MM(pu[:, 0:n], wg[:, k, jj * 256 + 128:jj * 256 + 256], hT[:, k, t0:t0 + n], k == 0, k == 7,
                                   [wgb, Bh[k][tc]], [pub])
                            si = (j + tc) % 2
                            ACT(sgf[si][:, 0:n], pg[:, 0:n], AF.Silu, [pgb], [Bsg[si]])
                            TT("dve", hd[:, j, t0:t0 + n], sgf[si][:, 0:n], pu[:, 0:n], ALU.mult, [Bsg[si], pub], [Bhd[j][tc]])
                for f in range(8):
                    for tc, (t0, n) in enumerate(TCS):
                        s = 1 if tc == 4 else 0
                        p_, pb_ = nextps()
                        for j in range(nj):
                            MM(p_[:, 0:n], wdv[:, j, f * 128:(f + 1) * 128], hd[:, j, t0:t0 + n], j == 0, j == nj - 1,
                               [wdb, Bhd[j][tc]], [pb_])
                        STT(xT[:, f, t0:t0 + n], p_[:, 0:n], modv[:, col(144, f, s):col(144, f, s) + 1], xT[:, f, t0:t0 + n],
                            ALU.mult, ALU.add, [pb_, Bmod, Bx[f][tc]], [Bx[f][tc]])
                if nxt:
                    mod_piece(l + 1, 2 * pi, nmps, nmpb)
                    mod_piece(l + 1, 2 * pi + 1, nmps, nmpb)
            if nxt:
                mod_finish(l + 1, nmps, nmpb)
            if nxt:
                mk_ffn = S.marks()
                nmodv = modv_all[:, ((l + 1) % 2) * 192:((l + 1) % 2) * 192 + 192]
                layernorm(64, 72, True, nmodv, Bmods[(l + 1) % 2], 16, 64)
                mk_ln = S.marks()
                S.fence(mk_ffn)
                return mk_ln
            layernorm(64, 72, False)
            S.barrier()
            return None

        mps0, mpb0 = accps()
        load_x(mps0, mpb0)
        S.barrier()
        mod_finish(0, mps0, mpb0)
        lnm = None
        for l in range(nl):
            lnm = layer(l, lnm)

        osb = [av(0, 4096, F32), av(4096, 4096, F32)]
        Bos = [Buf("os0"), Buf("os1")]
        Bout = Buf("out")
        for t in range(16):
            tc = t // 4
            o_, ob_ = osb[t % 2], Bos[t % 2]
            for cg in range(2):
                p_, pb_ = nextps()
                for i in range(4):
                    c = cg * 4 + i
                    S.add("pe", (lambda oo, ii: (lambda e: e.transpose(out=oo, in_=ii, identity=ident[:, :])))(
                        p_[:, i * 128:(i + 1) * 128], xT[:, c, t * 128:(t + 1) * 128]),
                        reads=[Bx[c][tc], Bident], writes=[pb_])
                if cg == 0:
                    ACT(o_[:, 0:512], p_[:, 0:512], AF.Copy, [pb_], [ob_])
                else:
                    COPY("dve", o_[:, 512:1024], p_[:, 0:512], [pb_], [ob_])
            DMA("sp", out_d[t * 128:(t + 1) * 128, :], o_, [ob_], [Bout, ob_])
        S.add("sp", lambda e: e.nop(), reads=[Bout])
        S.emit(st)
    return nc


def _partner(n, half):
    j = np.arange(n)
    q = half // 2
    return np.where((j % half) < q, j + q, j - q)


def _rope_tables():
    t = np.arange(S_LAT)
    rows = (t // 64).astype(np.float32)
    cols = (t % 64).astype(np.float32)
    tabs = np.zeros((4, 128, S_LAT), np.float32)
    tabs[0] = 1.0
    tabs[2] = 1.0
    inv = (10000.0 ** (-np.arange(0, 32, 2, dtype=np.float32) / 32)).astype(np.float32)
    for p in range(128):
        j = p % 64
        pos = rows if j < 32 else cols
        jj = j % 32
        i = jj % 16
        ang = pos * inv[i]
        sign = -1.0 if jj < 16 else 1.0
        tabs[0, p] = np.cos(ang)
        tabs[1, p] = sign * np.sin(ang)
    inv = (10000.0 ** (-np.arange(0, 16, 2, dtype=np.float32) / 16)).astype(np.float32)
    for p in range(64, 96):
        j = p - 64
        pos = rows if j < 16 else cols
        jj = j % 16
        i = jj % 8
        ang = pos * inv[i]
        sign = -1.0 if jj < 8 else 1.0
        tabs[2, p] = np.cos(ang)
        tabs[3, p] = sign * np.sin(ang)
    return tabs


def _rpb_index():
    idx = np.full((21, 128, 128), 465, np.int64)
    ql = np.arange(128)
    kl = np.arange(128)
    for j in [2, 0, 1, 14, 15]:
        slot0, kbs = b_class(j)
        r = 2 * j + ql // 64
        qc = ql % 64
        r_start = np.clip(r - 4, 0, 24)
        c_start = np.clip(qc - 8, 0, 48)
        for si, kb in enumerate(kbs):
            kr = 2 * kb + kl // 64
            kc = kl % 64
            vr = (kr[:, None] >= r_start[None, :]) & (kr[:, None] < r_start[None, :] + 8)
            vc = (kc[:, None] >= c_start[None, :]) & (kc[:, None] < c_start[None, :] + 16)
            dr = kr[:, None] - r[None, :] + 7
            dc = np.clip(kc[:, None] - qc[None, :] + 15, 0, 30)
            flat = np.clip(dr, 0, 14) * 31 + dc
            idx[slot0 + si] = np.where(vr & vc, flat, 465)
    return idx


_CONST = {}


def _consts():
    if not _CONST:
        _CONST["rope"] = _rope_tables()
        _CONST["rpbidx"] = _rpb_index()
        cst = np.zeros((128, 512), np.float32)
        cst[:, 0:128] = np.eye(128, dtype=np.float32)
        k = np.arange(128)[:, None]
        q = np.arange(128)[None, :]
        cst[:, 128:256] = np.where(k >= q, 0.0, NEG)
        cst[:, 256:384] = np.where(k <= q, 0.0, NEG)
        cst[:, 384:512] = np.eye(128, dtype=np.float32)
        _CONST["cst"] = cst
    return _CONST


def _prep_shared(w_mod, b_mod, w_in, a_sink, b_rpb, c_q_norm, c_kv_norm, c_w_uq, c_w_ukv, d_ln_g, d_ln_b, d_ws, d_bs,
                 w_out, ln1_g, ln1_b, w_gu, w_down, ln2_g, ln2_b):
    C = _consts()
    f = np.float32
    ak0, av0, bk0, bv0, ckv0, ckr0, aq0, bq0, ccq0, du0, dv0 = 0, 128, 256, 512, 768, 896, 928, 1184, 1440, 1696, 1952
    p64 = _partner(64, 32)
    p32 = _partner(32, 16)
    cols = []
    ak = ak0 + np.arange(128)
    ak_sw = ak0 + np.concatenate([p64, 64 + p64])
    aqh = lambda h: aq0 + h * 64 + np.arange(64)
    aqh_sw = lambda h: aq0 + h * 64 + p64
    cols += [ak, ak_sw, np.concatenate([aqh(0), aqh(2)]), np.concatenate([aqh_sw(0), aqh_sw(2)])]
    cols += [np.concatenate([aqh(1), aqh(3)]), np.concatenate([aqh_sw(1), aqh_sw(3)]), av0 + np.arange(128)]
    for p in range(2):
        cols += [bk0 + p * 128 + np.arange(128), bq0 + p * 128 + np.arange(128), bv0 + p * 128 + np.arange(128)]
    cols += [ckv0 + np.arange(128), np.concatenate([ckv0 + 64 + np.arange(64), ckr0 + np.arange(32)]),
             np.concatenate([ckv0 + 64 + np.arange(64), ckr0 + p32])]
    cols += [ccq0 + np.arange(256), du0 + np.arange(256), dv0 + np.arange(256)]
    cols = np.concatenate(cols)
    assert cols.shape[0] == NIN
    w_in_p = np.ascontiguousarray(np.asarray(w_in, f)[:, :, cols])
    uq_sw = np.concatenate([np.concatenate([h * 96 + np.arange(64), h * 96 + 64 + p32]) for h in range(4)])
    w_uq_p = np.ascontiguousarray(np.concatenate([np.asarray(c_w_uq, f), np.asarray(c_w_uq, f)[:, :, uq_sw]], axis=2))
    ukv_cols = np.concatenate([np.concatenate([h * 128 + np.arange(64) for h in range(4)]),
                               np.concatenate([h * 128 + 64 + np.arange(64) for h in range(4)])])
    w_ukv_p = np.ascontiguousarray(np.asarray(c_w_ukv, f)[:, :, ukv_cols])
    wsT = np.ascontiguousarray(np.transpose(np.asarray(d_ws, f), (0, 3, 1, 2)).reshape(L, 128, 512))
    rows = np.arange(1024)
    arow = np.concatenate([h * 64 + np.arange(64) for h in (0, 2, 1, 3)])
    rows[:256] = arow
    w_out_p = np.ascontiguousarray(np.asarray(w_out, f)[:, rows, :])
    gu_cols = np.concatenate([np.concatenate([j * 128 + np.arange(128), FF + j * 128 + np.arange(128)]) for j in range(22)])
    w_gu_p = np.ascontiguousarray(np.asarray(w_gu, f)[:, :, gu_cols])
    prm = np.zeros((128, NPRM), f)
    fm = lambda v, n: np.asarray(v, f).reshape(n, 128).T
    for l in range(L):
        b = l * PL
        prm[:, b:b + 48] = fm(b_mod[l], 48)
        prm[:, b + 48:b + 56] = fm(ln1_g[l], 8)
        prm[:, b + 56:b + 64] = fm(ln1_b[l], 8)
        prm[:, b + 64:b + 72] = fm(ln2_g[l], 8)
        prm[:, b + 72:b + 80] = fm(ln2_b[l], 8)
        prm[:, b + 80:b + 84] = np.broadcast_to(np.asarray(a_sink[l], f)[None, :], (128, 4))
        prm[:, b + 84:b + 86] = fm(c_q_norm[l], 2)
        prm[:, b + 86:b + 87] = fm(c_kv_norm[l], 1)
    prm[:, L * PL + 0] = EPS / (ALPHA * ALPHA)
    prm[:, L * PL + 1] = EPS
    dvec1 = np.concatenate([np.asarray(d_ln_g, f), np.asarray(d_ln_b, f), np.asarray(d_bs, f).reshape(L, 512)], axis=1)
    dvec = np.ascontiguousarray(np.broadcast_to(dvec1[:, None, :], (L, 128, 1024)))
    rp = np.asarray(b_rpb, f).reshape(L, 4, 465)
    ext = np.concatenate([rp, np.full((L, 4, 1), NEG, f)], axis=2)
    tab = ext[:, :, C["rpbidx"]]
    rpbt = np.ascontiguousarray(np.transpose(tab, (0, 1, 3, 2, 4)).reshape(L, 4, 128, 21 * 128))
    return {"w_mod": np.ascontiguousarray(np.asarray(w_mod, f)), "w_in": w_in_p, "w_uq": w_uq_p, "w_ukv": w_ukv_p,
            "wsT": wsT, "w_out": w_out_p, "w_gu": w_gu_p, "w_down": np.ascontiguousarray(np.asarray(w_down, f)),
            "prm": prm, "dvec": dvec, "rpbt": rpbt, "rope": C["rope"], "cst": C["cst"]}


_NC_CACHE = {}


def kernel(x, c, ctx, c_ctx, w_mod, b_mod, w_in, a_sink, b_rpb, c_q_norm, c_kv_norm, c_w_uq, c_w_ukv,
           d_ln_g, d_ln_b, d_ws, d_bs, w_out, ln1_g, ln1_b, w_gu, w_down, ln2_g, ln2_b, _nl=L, _cores=8):
    shared = _prep_shared(w_mod, b_mod, w_in, a_sink, b_rpb, c_q_norm, c_kv_norm, c_w_uq, c_w_ukv, d_ln_g, d_ln_b,
                          d_ws, d_bs, w_out, ln1_g, ln1_b, w_gu, w_down, ln2_g, ln2_b)
    x = np.asarray(x, np.float32)
    ctx = np.asarray(ctx, np.float32)
    c = np.asarray(c, np.float32)
    c_ctx = np.asarray(c_ctx, np.float32)
    in_maps = []
    for b in range(_cores):
        m = dict(shared)
        m["x"] = np.ascontiguousarray(np.concatenate([x[b], ctx[b]], axis=0))
        cT = np.zeros((128, 16), np.float32)
        cT[:, 0::2] = c[b].reshape(8, 128).T
        cT[:, 1::2] = c_ctx.reshape(8, 128).T
        m["cT"] = cT
        in_maps.append(m)
    if _nl not in _NC_CACHE:
        _NC_CACHE[_nl] = build(_nl)
    res = run_bass_kernel_spmd(_NC_CACHE[_nl], in_maps, core_ids=list(range(_cores)))
    return np.stack([np.asarray(r["out"], np.float32) for r in res.results], axis=0)
```

```python
import numpy as np
from contextlib import ExitStack
import concourse.bass as bass
import concourse.mybir as mybir
from concourse.bass_utils import run_bass_kernel_spmd

F32 = mybir.dt.float32
BF16 = mybir.dt.bfloat16
AF = mybir.ActivationFunctionType
ALU = mybir.AluOpType

L = 4
D = 1024
S_LAT = 2048
NCTX = 256
T = 2304
NT = 18
FF = 2816
TCS = [(0, 512), (512, 512), (1024, 512), (1536, 512), (2048, 256)]
ALPHA = 8.0 ** 0.25
EPS = 1e-6
NIN = 2752
PL = 88
NPRM = L * PL + 8
NEG = -30000.0
SC_C = 96.0 ** -0.5

ENGS = ["pe", "act", "dve", "pool", "sp"]
SEM_LIMIT = 30000
N_DMA_SEMS = 24


class Buf:
    __slots__ = ("name", "w", "r")

    def __init__(self, name=""):
        self.name = name
        self.w = None
        self.r = []


class Op:
    __slots__ = ("eng", "fn", "waits", "signal", "idx", "sem", "val", "dma", "dma_prev")

    def __init__(self, eng, fn, dma):
        self.eng = eng
        self.fn = fn
        self.dma = dma
        self.waits = []
        self.signal = False
        self.sem = None
        self.val = None
        self.dma_prev = None


class Sched:
    def __init__(self, nc):
        self.nc = nc
        self.ops = {e: [] for e in ENGS}
        self.seen = {e: {p: -1 for p in ENGS} for e in ENGS}
        self.seen_dma = {e: set() for e in ENGS}
        self.dma_ops = []

    def add(self, eng, fn, reads=(), writes=(), dma=False):
        op = Op(eng, fn, dma)
        op.idx = len(self.ops[eng])
        deps = []
        for b in reads:
            if b.w is not None:
                deps.append(b.w)
        for b in writes:
            if b.w is not None:
                deps.append(b.w)
            deps.extend(b.r)
        for b in reads:
            b.r.append(op)
        for b in writes:
            b.w = op
            b.r = []
        best = {}
        for d in deps:
            if d is op:
                continue
            if d.dma:
                if d in self.seen_dma[eng]:
                    continue
                self.seen_dma[eng].add(d)
                op.waits.append(d)
                continue
            if d.eng == eng and eng == "pe":
                continue
            if d.idx <= self.seen[eng][d.eng]:
                continue
            if d.eng not in best or best[d.eng].idx < d.idx:
                best[d.eng] = d
        for p, d in best.items():
            self.seen[eng][p] = d.idx
            op.waits.append(d)
            d.signal = True
        if dma:
            op.signal = True
            n = len(self.dma_ops)
            if n >= N_DMA_SEMS:
                op.dma_prev = self.dma_ops[n - N_DMA_SEMS]
            self.dma_ops.append(op)
        self.ops[eng].append(op)
        return op

    def marks(self):
        out = []
        for e in ["pe", "act", "dve", "pool"]:
            for op in reversed(self.ops[e]):
                if not op.dma:
                    out.append(op)
                    break
        return out

    def fence(self, marks, engines=("pe", "act", "dve", "pool", "sp")):
        for e in engines:
            op = Op(e, lambda g: g.nop(), False)
            op.idx = len(self.ops[e])
            for d in marks:
                if d.eng == e and e == "pe":
                    continue
                if d.idx <= self.seen[e][d.eng]:
                    continue
                self.seen[e][d.eng] = d.idx
                op.waits.append(d)
                d.signal = True
            self.ops[e].append(op)

    def barrier(self):
        self.fence(self.marks(), engines=("pe", "act", "dve", "pool", "sp"))

    def emit(self, stack):
        nc = self.nc
        sems = {}
        for e in ENGS:
            cnt = 0
            for op in self.ops[e]:
                if op.dma or not op.signal:
                    continue
                key = (e, cnt // SEM_LIMIT)
                if key not in sems:
                    sems[key] = stack.enter_context(nc.semaphore(f"s_{e}_{key[1]}"))
                op.sem = sems[key]
                op.val = cnt % SEM_LIMIT + 1
                cnt += 1
        dsems = [stack.enter_context(nc.semaphore(f"s_dma_{i}")) for i in range(N_DMA_SEMS)]
        for i, op in enumerate(self.dma_ops):
            op.sem = dsems[i % N_DMA_SEMS]
            op.val = 16 * (i // N_DMA_SEMS + 1)
        block = stack.enter_context(nc.Block())

        def run(engobj, ops):
            for op in ops:
                if op.dma_prev is not None:
                    engobj.wait_ge(op.dma_prev.sem, op.dma_prev.val)
                for d in op.waits:
                    engobj.wait_ge(d.sem, d.val)
                ins = op.fn(engobj)
                if op.signal:
                    ins.then_inc(op.sem, 16 if op.dma else 1)

        @block.tensor
        def _(e):
            run(e, self.ops["pe"])

        @block.scalar
        def _(e):
            run(e, self.ops["act"])

        @block.vector
        def _(e):
            run(e, self.ops["dve"])

        @block.gpsimd
        def _(e):
            run(e, self.ops["pool"])

        @block.sync
        def _(e):
            run(e, self.ops["sp"])


def b_class(j):
    if 2 <= j <= 13:
        return 0, list(range(j - 2, j + 3))
    if j == 0:
        return 5, [0, 1, 2, 3]
    if j == 1:
        return 9, [0, 1, 2, 3]
    if j == 14:
        return 13, [12, 13, 14, 15]
    return 17, [12, 13, 14, 15]


def build(nl=L):
    import os
    SKIP = os.environ.get('KSKIP', '')
    nc = bass.Bass("TRN2", target_bir_lowering=False)

    def din(name, shape):
        return nc.dram_tensor(name, shape, F32, kind="ExternalInput").ap()

    x_d = din("x", [T, D])
    cT_d = din("cT", [128, 16])
    wmod_d = din("w_mod", [L, D, 6 * D])
    win_d = din("w_in", [L, D, NIN])
    wuq_d = din("w_uq", [L, 256, 768])
    wukv_d = din("w_ukv", [L, 128, 512])
    wsT_d = din("wsT", [L, 128, 512])
    wout_d = din("w_out", [L, D, D])
    wgu_d = din("w_gu", [L, D, 2 * FF])
    wdn_d = din("w_down", [L, FF, D])
    prm_d = din("prm", [128, NPRM])
    dvec_d = din("dvec", [L, 128, 1024])
    rpbt_d = din("rpbt", [L, 4, 128, 21 * 128])
    rope_d = din("rope", [4, 128, S_LAT])
    cst_d = din("cst", [128, 512])
    out_d = nc.dram_tensor("out", [S_LAT, D], F32, kind="ExternalOutput").ap()

    S = Sched(nc)
    with ExitStack() as st:
        def sbuf(name, shape, dt):
            return st.enter_context(nc.sbuf_tensor("sb_" + name, shape, dt))

        xT = sbuf("xT", [128, 8, T], F32)
        hT = sbuf("hT", [128, 8, T], BF16)
        wts = [sbuf(f"wt{i}", [128, 4096], BF16) for i in range(5)]
        ident = sbuf("ident", [128, 128], F32)
        cbf = sbuf("cbf", [128, 512], BF16)
        prm = sbuf("prm", [128, NPRM], F32)
        modv_all = sbuf("modv", [128, 384], F32)
        Bmods = [Buf("modv0"), Buf("modv1")]
        misc = sbuf("misc", [128, 64], F32)
        sT = sbuf("sT", [128, 16], BF16)
        cT = sbuf("cTs", [128, 16], F32)
        rowsb = sbuf("rowsb", [1, 1024], BF16)
        ARB = 27136
        arena = sbuf("arena", [128, ARB], BF16)
        psq = [st.enter_context(nc.psum_tensor(f"psq{i}", [128, 1024], F32)) for i in range(4)]
        ps = [psq[i // 2][:, (i % 2) * 512:(i % 2) * 512 + 512] for i in range(8)]
        psb = [Buf(f"ps{i}") for i in range(8)]

        onesb = cbf[:, 0:128]
        maskA = cbf[:, 128:384]
        identb = cbf[:, 384:512]

        rot = {"ps": 0, "acc": 0, "wt": 0}

        def nextps():
            i = rot["ps"] % 6
            rot["ps"] += 1
            return ps[i], psb[i]

        def nextps2():
            if rot["ps"] % 2:
                rot["ps"] += 1
            i = rot["ps"] % 6
            rot["ps"] += 2
            return psq[i // 2], psb[i], psb[i + 1]

        def accps():
            i = 6 + rot["acc"] % 2
            rot["acc"] += 1
            return ps[i], psb[i]

        wtb = [Buf(f"wt{i}") for i in range(5)]

        def nextwt():
            i = rot["wt"] % 5
            rot["wt"] += 1
            return wts[i], wtb[i]

        def MM(out, lhsT, rhs, start, stop, R, W):
            S.add("pe", lambda e: e.matmul(out, lhsT=lhsT, rhs=rhs, start=start, stop=stop), reads=R, writes=W)

        def ACT(out, in_, func, R, W, **kw):
            S.add("act", lambda e: e.activation(out=out, in_=in_, func=func, **kw), reads=R, writes=W)

        def TT(eng, out, in0, in1, op, R, W):
            S.add(eng, lambda e: e.tensor_tensor(out=out, in0=in0, in1=in1, op=op), reads=R, writes=W)

        def STT(out, in0, scalar, in1, op0, op1, R, W):
            S.add("dve", lambda e: e.scalar_tensor_tensor(out=out, in0=in0, scalar=scalar, in1=in1, op0=op0, op1=op1),
                  reads=R, writes=W)

        def TS(eng, out, in0, s1, op0, R, W, s2=None, op1=None):
            if op1 is None:
                S.add(eng, lambda e: e.tensor_single_scalar(out=out, in_=in0, scalar=s1, op=op0), reads=R, writes=W)
            else:
                S.add(eng, lambda e: e.tensor_scalar(out=out, in0=in0, scalar1=s1, scalar2=s2, op0=op0, op1=op1), reads=R, writes=W)

        def RECIP(out, in_, R, W):
            S.add("dve", lambda e: e.reciprocal(out=out, in_=in_), reads=R, writes=W)

        def COPY(eng, out, in_, R, W):
            S.add(eng, lambda e: e.tensor_copy(out=out, in_=in_), reads=R, writes=W)

        def MEMSET(eng, ap, val, W):
            S.add(eng, lambda e: e.memset(ap, val), writes=W)

        def DMA(eng, out, in_, R, W):
            S.add(eng, lambda e: e.dma_start(out=out, in_=in_), reads=R, writes=W, dma=True)

        def av(off_b, nbytes, dt):
            a = arena[:, off_b // 2:(off_b + nbytes) // 2]
            if dt == F32:
                return a.bitcast(F32)
            return a

        Bx = [[Buf(f"x{c}_{tc}") for tc in range(5)] for c in range(8)]
        Bh = [[Buf(f"h{c}_{tc}") for tc in range(5)] for c in range(8)]
        Bconst = Buf("const")
        Bprm = Buf("prm")
        Bmisc = Buf("misc")
        BsT = Buf("sT")
        Brows = Buf("rowsb")
        Bident = Buf("ident")

        DMA("sp", ident[:, :], cst_d[:, 0:128], [], [Bident])
        DMA("sp", prm[:, :], prm_d[:, :], [], [Bprm])
        DMA("sp", cT[:, :], cT_d[:, :], [], [BsT])
        DMA("pool", cbf[:, 128:512], cst_d[:, 128:512], [], [Bconst])
        MEMSET("pool", cbf[:, 0:128], 1.0, [Bconst])
        MEMSET("pool", rowsb[:, :], 1.0, [Brows])
        ACT(sT[:, :], cT[:, :], AF.Silu, [BsT], [BsT])

        xin = [av(0, 4096, F32), av(4096, 4096, F32)]
        Bxin = [Buf("xin0"), Buf("xin1")]

        def load_x(mps0, mpb0):
          for t in range(NT):
              xi, bxi = xin[t % 2], Bxin[t % 2]
              DMA("sp", xi, x_d[t * 128:(t + 1) * 128, :], [], [bxi])
              tc = min(t // 4, 4)
              for cg in range(2):
                  p_, pb_ = nextps()
                  for i in range(4):
                      c = cg * 4 + i
                      S.add("pe", (lambda o_, i_: (lambda e: e.transpose(out=o_, in_=i_, identity=ident[:, :])))(
                          p_[:, i * 128:(i + 1) * 128], xi[:, c * 128:(c + 1) * 128]),
                          reads=[bxi, Bident], writes=[pb_])
                  outv = xT[:, cg * 4:cg * 4 + 4, t * 128:(t + 1) * 128]
                  inv = p_[:, :].rearrange("p (a b) -> p a b", b=128)
                  if cg == 0:
                      ACT(outv, inv, AF.Copy, [pb_], [Bx[cg * 4 + i][tc] for i in range(4)])
                  else:
                      COPY("dve", outv, inv, [pb_], [Bx[cg * 4 + i][tc] for i in range(4)])
              if t < 12:
                  mod_piece(0, t, mps0, mpb0)


        def col(base, c, s):
            return base + 2 * c + s

        def load_w(src_ap, kc, ncols):
            wt, wb = nextwt()
            view = wt[:, 0:kc * ncols].rearrange("p (k n) -> p k n", n=ncols)
            DMA("pool", view, src_ap.rearrange("(k p) n -> p k n", p=128), [], [wb])
            return view, wb

        def mod_piece(l, i, mod_ps, mod_pb):
            wv, wb = load_w(wmod_d[l, :, i * 512:(i + 1) * 512], 8, 512)
            for oc in range(4):
                o = (i * 4 + oc) * 2
                for k in range(8):
                    MM(mod_ps[:, o:o + 2], wv[:, k, oc * 128:(oc + 1) * 128], sT[:, 2 * k:2 * k + 2], k == 0, k == 7,
                       [wb, BsT], [mod_pb])

        def mod_finish(l, mod_ps, mod_pb):
            pb = l * PL
            modv = modv_all[:, (l % 2) * 192:(l % 2) * 192 + 192]
            Bmod = Bmods[l % 2]
            TT("dve", modv[:, 0:96].rearrange("p (a s) -> p a s", s=2), mod_ps[:, 0:96].rearrange("p (a s) -> p a s", s=2),
               prm[:, pb:pb + 48].unsqueeze(2).broadcast_to([128, 48, 2]), ALU.add, [mod_pb, Bprm], [Bmod])
            TS("dve", modv[:, 96:112], modv[:, 16:32], 1.0, ALU.add, [Bmod], [Bmod])
            TS("dve", modv[:, 112:128], modv[:, 32:48], 1.0 / ALPHA, ALU.mult, [Bmod], [Bmod])
            TS("dve", modv[:, 128:144], modv[:, 64:80], 1.0, ALU.add, [Bmod], [Bmod])
            TS("dve", modv[:, 144:160], modv[:, 80:96], 1.0 / ALPHA, ALU.mult, [Bmod], [Bmod])
            g1 = prm[:, pb + 48:pb + 56].unsqueeze(2).broadcast_to([128, 8, 2])
            b1 = prm[:, pb + 56:pb + 64].unsqueeze(2).broadcast_to([128, 8, 2])
            v3 = lambda a, b: modv[:, a:b].rearrange("p (c s) -> p c s", s=2)
            TT("dve", v3(160, 176), v3(128, 144), g1, ALU.mult, [Bmod, Bprm], [Bmod])
            TT("dve", v3(176, 192), v3(128, 144), b1, ALU.mult, [Bmod, Bprm], [Bmod])
            TT("dve", v3(176, 192), v3(176, 192), v3(48, 64), ALU.add, [Bmod], [Bmod])
            if l >= 1:
                pp = (l - 1) * PL
                g2 = prm[:, pp + 64:pp + 72].unsqueeze(2).broadcast_to([128, 8, 2])
                b2 = prm[:, pp + 72:pp + 80].unsqueeze(2).broadcast_to([128, 8, 2])
                TT("dve", v3(16, 32), v3(96, 112), g2, ALU.mult, [Bmod, Bprm], [Bmod])
                TT("dve", v3(64, 80), v3(96, 112), b2, ALU.mult, [Bmod, Bprm], [Bmod])
                TT("dve", v3(64, 80), v3(64, 80), v3(0, 16), ALU.add, [Bmod], [Bmod])
            ACT(misc[:, 0:4], prm[:, pb + 80:pb + 84], AF.Exp, [Bprm], [Bmisc])
            MEMSET("dve", rowsb[:, 0:512], 0.0, [Brows])
            for h in range(4):
                hh = h // 2
                other = 1 - hh
                COPY("dve", rowsb[0:1, h * 128 + other * 64:h * 128 + other * 64 + 64],
                     misc[0:1, h:h + 1].broadcast_to([1, 64]), [Bmisc], [Brows])

        def layer(l, ln_marks):
            pb = l * PL
            modv = modv_all[:, (l % 2) * 192:(l % 2) * 192 + 192]
            Bmod = Bmods[l % 2]

            for c in range(8 if l == 0 else 0):
                for tc, (t0, n) in enumerate(TCS):
                    s = 1 if tc == 4 else 0
                    ACT(hT[:, c, t0:t0 + n], xT[:, c, t0:t0 + n], AF.Identity, [Bx[c][tc], Bmod], [Bh[c][tc]],
                        scale=modv[:, col(96, c, s):col(96, c, s) + 1], bias=modv[:, col(0, c, s):col(0, c, s) + 1])

            yT = av(0, 9216, BF16).rearrange("p (c t) -> p c t", t=T)
            By = [[Buf(f"y{c}_{tc}") for tc in range(5)] for c in range(2)]
            pts = [av(9216 + i * 1024, 1024, BF16) for i in range(4)]
            Bpt = [Buf(f"pt{i}") for i in range(4)]
            tmpf = [av(13312 + i * 2048, 2048, F32) for i in range(2)]
            Btmp = [Buf(f"tmpf{i}") for i in range(2)]
            rcs = av(17408, 2048, F32)
            Brc = Buf("rc")
            MS = 20480
            rp = {"pt": 0}

            def nextpt():
                i = rp["pt"] % 4
                rp["pt"] += 1
                return pts[i], Bpt[i]

            def inproj_fm(wv, wb, c0, m, tc):
                t0, n = TCS[tc]
                p_, pb_ = nextps()
                for k in range(8):
                    MM(p_[0:m, 0:n], wv[:, k, c0:c0 + m], hT[:, k, t0:t0 + n], k == 0, k == 7, [wb, Bh[k][tc]], [pb_])
                return p_, pb_

            def v_tm(wv, wb, c0, Vt, Bv):
                for tc, (t0, n) in enumerate(TCS):
                    nt = n // 128
                    p_, pb_ = nextps()
                    for i in range(nt):
                        t = t0 // 128 + i
                        for k in range(8):
                            MM(p_[:, i * 128:(i + 1) * 128], hT[:, k, t * 128:(t + 1) * 128], wv[:, k, c0:c0 + 128],
                               k == 0, k == 7, [wb, Bh[k][tc]], [pb_])
                    pv = p_[:, 0:nt * 128].rearrange("p (a b) -> p a b", b=128)
                    tt = t0 // 128
                    ACT(Vt[:, tt:tt + nt, 0:64], pv[:, :, 0:64], AF.Copy, [pb_], [Bv[tc]])
                    COPY("dve", Vt[:, tt:tt + nt, 128:192], pv[:, :, 64:128], [pb_], [Bv[tc]])

            def normalize(O, Ob, hh, ydst, By_w, n):
                rows = slice(hh * 64, hh * 64 + 64)
                oth = slice((1 - hh) * 64, (1 - hh) * 64 + 64)
                RECIP(rcs[rows, 0:n], O[oth, 0:n], [Ob], [Brc])
                TT("dve", ydst, O[rows, 0:n], rcs[rows, 0:n], ALU.mult, [Ob, Brc], By_w)

            def outproj(mi):
                if 'abcd'[mi] in SKIP:
                    return
                wv, wb = load_w(wout_d[l, mi * 256:(mi + 1) * 256, :], 2, 1024)
                for f in range(8):
                    for tc, (t0, n) in enumerate(TCS):
                        s = 1 if tc == 4 else 0
                        p_, pb_ = nextps()
                        MM(p_[:, 0:n], wv[:, 0, f * 128:(f + 1) * 128], yT[:, 0, t0:t0 + n], True, False, [wb, By[0][tc]], [pb_])
                        MM(p_[:, 0:n], wv[:, 1, f * 128:(f + 1) * 128], yT[:, 1, t0:t0 + n], False, True, [wb, By[1][tc]], [pb_])
                        STT(xT[:, f, t0:t0 + n], p_[:, 0:n], modv[:, col(112, f, s):col(112, f, s) + 1], xT[:, f, t0:t0 + n],
                            ALU.mult, ALU.add, [pb_, Bmod, Bx[f][tc]], [Bx[f][tc]])

            def load_rope(which, tc, cos_t, sin_t, Bct):
                t0, n = TCS[tc]
                DMA("sp", cos_t, rope_d[which, :, t0:t0 + n], [], [Bct])
                DMA("sp", sin_t, rope_d[which + 1, :, t0:t0 + n], [], [Bct])

            for p in range(2):
                if p == 1 and ln_marks is not None:
                    S.fence(ln_marks)
                base = MS + p * 16128
                bkT = av(base, 4608, BF16)
                bqT = av(base + 4608, 4608, BF16)
                Vb = av(base + 9216, 6912, BF16).rearrange("p (t c) -> p t c", c=192)
                Bbk = [Buf(f"bk{tc}") for tc in range(5)]
                Bbq = [Buf(f"bq{tc}") for tc in range(5)]
                Bvb = [Buf(f"vb{tc}") for tc in range(5)]
                wB, wbB = load_w(win_d[l, :, 896 + p * 384:896 + (p + 1) * 384], 8, 384)
                tabs = []
                for hh in range(2):
                    wt, wb = nextwt()
                    DMA("pool", wt[:, 0:21 * 128], rpbt_d[l, 2 * p + hh, :, :], [], [wb])
                    tabs.append((wt, wb))
                MEMSET("pool", Vb[:, :, 64:128], 1.0, Bvb)
                for tc, (t0, n) in enumerate(TCS):
                    p1, pb1 = inproj_fm(wB, wbB, 0, 128, tc)
                    ACT(bkT[:, t0:t0 + n], p1[:, 0:n], AF.Copy, [pb1], [Bbk[tc]])
                    p2, pb2 = inproj_fm(wB, wbB, 128, 128, tc)
                    ACT(bqT[:, t0:t0 + n], p2[:, 0:n], AF.Identity, [pb2], [Bbq[tc]], scale=0.125)
                v_tm(wB, wbB, 256, Vb, Bvb)
                for hh in range(2):
                    rows = slice(hh * 64, hh * 64 + 64)
                    tab, tabb = tabs[hh]
                    def b_scores(j):
                        qtc = j // 4
                        slot0, kbs = b_class(j)
                        pA, pAb = nextps()
                        pB, pBb = nextps()
                        qsl = bqT[rows, j * 128:(j + 1) * 128]
                        MM(pA[:, 0:512], identb, tab[:, slot0 * 128:(slot0 + 4) * 128], True, False, [Bconst, tabb], [pAb])
                        for si, kb in enumerate(kbs):
                            if si < 4:
                                MM(pA[:, si * 128:(si + 1) * 128], bkT[rows, kb * 128:(kb + 1) * 128], qsl, False, si == 3,
                                   [Bbk[kb // 4], Bbq[qtc]], [pAb])
                            else:
                                MM(pB[:, 0:128], bkT[rows, kb * 128:(kb + 1) * 128], qsl, True, False, [Bbk[kb // 4], Bbq[qtc]], [pBb])
                                MM(pB[:, 0:128], identb, tab[:, (slot0 + 4) * 128:(slot0 + 5) * 128], False, True, [Bconst, tabb], [pBb])
                        for ci, kb in enumerate((16, 17)):
                            MM(pB[:, (1 + ci) * 128:(2 + ci) * 128], bkT[rows, kb * 128:(kb + 1) * 128], qsl, True, True,
                               [Bbk[4], Bbq[qtc]], [pBb])
                        return pA, pAb, pB, pBb, kbs

                    def b_rest(st_, O, Ob, jj):
                        pA, pAb, pB, pBb, kbs = st_
                        P1, P1b = nextpt()
                        P2, P2b = nextpt()
                        ACT(P1[:, 0:512], pA[:, 0:512], AF.Exp, [pAb], [P1b])
                        lo = 0 if len(kbs) == 5 else 128
                        ACT(P2[:, lo:384], pB[:, lo:384], AF.Exp, [pBb], [P2b])
                        oc = O[:, jj * 128:(jj + 1) * 128]
                        for si, kb in enumerate(kbs):
                            src, sb_ = (P1[:, si * 128:(si + 1) * 128], P1b) if si < 4 else (P2[:, 0:128], P2b)
                            MM(oc, Vb[:, kb, hh * 64:hh * 64 + 128], src, si == 0, False, [Bvb[kb // 4], sb_], [Ob])
                        for ci, kb in enumerate((16, 17)):
                            MM(oc, Vb[:, kb, hh * 64:hh * 64 + 128], P2[:, (1 + ci) * 128:(2 + ci) * 128], False, ci == 1,
                               [Bvb[4], P2b], [Ob])

                    pend = b_scores(0)
                    for j in range(16):
                        qtc, jj = divmod(j, 4)
                        q0, n = TCS[qtc]
                        if jj == 0:
                            O, Ob = accps()
                        st_ = pend
                        if j + 1 < 16:
                            pend = b_scores(j + 1)
                        b_rest(st_, O, Ob, jj)
                        if jj == 3:
                            normalize(O, Ob, hh, yT[rows, p, q0:q0 + n], [By[p][qtc]], n)
                    q0, n = TCS[4]
                    O, Ob = accps()
                    for ci, kb in enumerate((16, 17)):
                        sp_, spb = nextps()
                        MM(sp_[:, 0:n], bkT[rows, kb * 128:(kb + 1) * 128], bqT[rows, q0:q0 + n], True, True, [Bbk[4], Bbq[4]], [spb])
                        P, Pb = nextpt()
                        ACT(P[:, 0:n], sp_[:, 0:n], AF.Exp, [spb], [Pb])
                        MM(O[:, 0:n], Vb[:, kb, hh * 64:hh * 64 + 128], P[:, 0:n], ci == 0, ci == 1, [Bvb[4], Pb], [Ob])
                    normalize(O, Ob, hh, yT[rows, p, q0:q0 + n], [By[p][4]], n)
            mk = S.marks()
            outproj(1)
            S.fence(mk)

            akT = av(MS, 4608, BF16)
            aqT = av(MS + 4608, 9216, BF16).rearrange("p (c t) -> p c t", t=T)
            Va = av(MS + 13824, 6912, BF16).rearrange("p (t c) -> p t c", c=192)
            ropeA = [[av(MS + 20736 + (2 * i + j) * 2048, 2048, F32) for j in range(2)] for i in range(2)]
            Bak = [Buf(f"ak{tc}") for tc in range(5)]
            Baq = [[Buf(f"aq{c}_{tc}") for tc in range(5)] for c in range(2)]
            Bva = [Buf(f"va{tc}") for tc in range(5)]
            Bra = [Buf("ropeA0"), Buf("ropeA1")]
            wA1, wbA1 = load_w(win_d[l, :, 0:512], 8, 512)
            wA2, wbA2 = load_w(win_d[l, :, 512:896], 8, 384)
            MEMSET("pool", Va[:, :, 64:128], 1.0, Bva)
            for tc, (t0, n) in enumerate(TCS):
                lat = tc < 4
                if lat:
                    cos_t, sin_t = ropeA[tc % 2]
                    load_rope(0, tc, cos_t, sin_t, Bra[tc % 2])
                for (wv, wb, c0, dst, bd) in [(wA1, wbA1, 0, akT[:, t0:t0 + n], Bak[tc]),
                                              (wA1, wbA1, 256, aqT[:, 0, t0:t0 + n], Baq[0][tc]),
                                              (wA2, wbA2, 0, aqT[:, 1, t0:t0 + n], Baq[1][tc])]:
                    p1, pb1 = inproj_fm(wv, wb, c0, 128, tc)
                    if lat:
                        p2, pb2 = inproj_fm(wv, wb, c0 + 128, 128, tc)
                        TT("dve", tmpf[0][:, 0:n], p1[:, 0:n], cos_t[:, 0:n], ALU.mult, [pb1, Bra[tc % 2]], [Btmp[0]])
                        TT("dve", tmpf[1][:, 0:n], p2[:, 0:n], sin_t[:, 0:n], ALU.mult, [pb2, Bra[tc % 2]], [Btmp[1]])
                        TT("pool", dst, tmpf[0][:, 0:n], tmpf[1][:, 0:n], ALU.add, [Btmp[0], Btmp[1]], [bd])
                    else:
                        ACT(dst, p1[:, 0:n], AF.Copy, [pb1], [bd])
            v_tm(wA2, wbA2, 256, Va, Bva)

            def attn_a_chunk(cq, hh, qtc):
                h = 2 * hh + cq
                rows = slice(hh * 64, hh * 64 + 64)
                q0, n = TCS[qtc]
                O, Ob = accps()
                steps = [(kb, 0, n, q0, []) for kb in (16, 17)]
                if qtc < 4:
                    n0 = qtc * 4
                    for kb in range(max(0, n0 - 1), min(15, n0 + 4) + 1):
                        qlo = max(n0, kb - 1)
                        qhi = min(n0 + 3, kb + 1)
                        masks = []
                        for nq in range(qlo, qhi + 1):
                            if nq == kb + 1:
                                masks.append(((nq - n0) * 128, 0))
                            elif nq == kb - 1:
                                masks.append(((nq - n0) * 128, 1))
                        steps.append((kb, (qlo - n0) * 128, (qhi - qlo + 1) * 128, qlo * 128, masks))

                def score(st_):
                    kb, coff, w, qoff, masks = st_
                    sp_, spb = nextps()
                    MM(sp_[:, coff:coff + w], akT[rows, kb * 128:(kb + 1) * 128], aqT[rows, cq, qoff:qoff + w], True,
                       len(masks) == 0, [Bak[min(kb // 4, 4)], Baq[cq][qtc]], [spb])
                    for mi_, (co, m) in enumerate(masks):
                        MM(sp_[:, co:co + 128], identb, maskA[:, m * 128:(m + 1) * 128], False, mi_ == len(masks) - 1,
                           [Bconst], [spb])
                    return sp_, spb

                pend = [score(steps[0])]
                if len(steps) > 1:
                    pend.append(score(steps[1]))
                for si, st_ in enumerate(steps):
                    kb, coff, w, qoff, masks = st_
                    sp_, spb = pend.pop(0)
                    if si + 2 < len(steps):
                        pend.append(score(steps[si + 2]))
                    P, Pb = nextpt()
                    ACT(P[:, coff:coff + w], sp_[:, coff:coff + w], AF.Exp, [spb], [Pb], scale=0.125)
                    MM(O[:, coff:coff + w], Va[:, kb, hh * 64:hh * 64 + 128], P[:, coff:coff + w], si == 0, False,
                       [Bva[min(kb // 4, 4)], Pb], [Ob])
                MM(O[:, 0:n], rowsb[0:1, h * 128:(h + 1) * 128], rowsb[0:1, 512:512 + n], False, True, [Brows], [Ob])
                normalize(O, Ob, hh, yT[rows, cq, q0:q0 + n], [By[cq][qtc]], n)

            for cq in range(2):
                for hh in range(2):
                    for qtc in range(5):
                        attn_a_chunk(cq, hh, qtc)
            mk = S.marks()
            outproj(0)
            S.fence(mk)

            uT = av(MS, 18432, F32).rearrange("p (c t) -> p c t", t=T)
            vd = av(MS + 18432, 9216, BF16).rearrange("p (t c) -> p t c", c=256)
            prb = av(MS + 27648, 4096, F32)
            dtm = [av(MS + 31744 + i * 1024, 1024, F32) for i in range(2)]
            Bu = [[Buf(f"u{c}_{tc}") for tc in range(5)] for c in range(2)]
            Bvd = [Buf(f"vd{tc}") for tc in range(5)]
            Bprb = Buf("prb")
            Bdt = [Buf("dtm0"), Buf("dtm1")]
            Bst = Buf("dstat")
            wD1, wbD1 = load_w(win_d[l, :, 2240:2496], 8, 256)
            wD2, wbD2 = load_w(win_d[l, :, 2496:2752], 8, 256)
            wsv, wbs = load_w(wsT_d[l, :, :], 1, 512)
            wsT = wsv[:, 0, :]
            DMA("sp", prb[:, :], dvec_d[l, :, :], [], [Bprb])
            epsd = prm[:, L * PL + 1:L * PL + 2]
            for c in range(2):
                for tc, (t0, n) in enumerate(TCS):
                    p1, pb1 = inproj_fm(wD1, wbD1, c * 128, 128, tc)
                    ACT(uT[:, c, t0:t0 + n], p1[:, 0:n], AF.Gelu_apprx_tanh, [pb1], [Bu[c][tc]])
            mvall = rcs[:, 0:36]
            rsall = rcs[:, 64:82]
            nball = rcs[:, 96:114]
            Bst2 = [Buf("dstat0"), Buf("dstat1")]

            def dv_gelu(t):
                tc = min(t // 4, 4)
                p_, pb_ = nextps()
                for k in range(8):
                    MM(p_[:, 0:256], hT[:, k, t * 128:(t + 1) * 128], wD2[:, k, 0:256], k == 0, k == 7, [wbD2, Bh[k][tc]], [pb_])
                ACT(dtm[t % 2][:, :], p_[:, 0:256], AF.Gelu_apprx_tanh, [pb_], [Bdt[t % 2]])

            for t in range(NT):
                par = t % 2
                st6 = misc[:, 8 + 8 * par:14 + 8 * par]
                dv_gelu(t)
                S.add("dve", (lambda a_, b_: (lambda e: e.bn_stats(out=a_, in_=b_)))(st6, dtm[par][:, :]), reads=[Bdt[par]], writes=[Bst2[par]])
                S.add("dve", (lambda a_, b_: (lambda e: e.bn_aggr(out=a_, in_=b_)))(mvall[:, 2 * t:2 * t + 2], st6), reads=[Bst2[par]],
                      writes=[Bst2[par], Brc])
            mv3 = mvall.rearrange("p (t two) -> p t two", two=2)
            ACT(rsall, mv3[:, :, 1], AF.Sqrt, [Brc, Bprm], [Brc], scale=1.0, bias=epsd)
            RECIP(rsall, rsall, [Brc], [Brc])
            STT(nball, mv3[:, :, 0], -1.0, rsall, ALU.mult, ALU.mult, [Brc], [Brc])
            for t in range(NT):
                tc = min(t // 4, 4)
                par = t % 2
                dv_gelu(t)
                ACT(dtm[par][:, :], dtm[par][:, :], AF.Identity, [Bdt[par], Brc], [Bdt[par]], scale=rsall[:, t:t + 1], bias=nball[:, t:t + 1])
                TT("dve", dtm[par][:, :], dtm[par][:, :], prb[:, 0:256], ALU.mult, [Bdt[par], Bprb], [Bdt[par]])
                TT("dve", vd[:, t, :], dtm[par][:, :], prb[:, 256:512], ALU.add, [Bdt[par], Bprb], [Bvd[tc]])
            for gp in range(2):
                for tc, (t0, n) in enumerate(TCS):
                    nt = n // 128
                    pa, pab = nextps()
                    pb2, pbb = nextps()
                    for i in range(nt):
                        t = t0 // 128 + i
                        MM(pa[:, i * 128:(i + 1) * 128], vd[:, t, gp * 128:(gp + 1) * 128], wsT[:, (2 * gp) * 128:(2 * gp + 1) * 128],
                           True, True, [Bvd[tc], wbs], [pab])
                        MM(pb2[:, i * 128:(i + 1) * 128], vd[:, t, gp * 128:(gp + 1) * 128],
                           wsT[:, (2 * gp + 1) * 128:(2 * gp + 2) * 128], True, True, [Bvd[tc], wbs], [pbb])
                    for hh, (pp, ppb) in enumerate([(pa, pab), (pb2, pbb)]):
                        g = 2 * gp + hh
                        rows = slice(hh * 64, hh * 64 + 64)
                        bsb = prb[rows, 512 + g * 128:512 + (g + 1) * 128].unsqueeze(1).broadcast_to([64, nt, 128])
                        TT("dve", tmpf[hh][rows, 0:n].rearrange("p (a b) -> p a b", b=128),
                           pp[rows, 0:n].rearrange("p (a b) -> p a b", b=128), bsb, ALU.add, [ppb, Bprb], [Btmp[hh]])
                        TT("dve", yT[rows, gp, t0:t0 + n], tmpf[hh][rows, 0:n], uT[rows, gp, t0:t0 + n], ALU.mult,
                           [Btmp[hh], Bu[gp][tc]], [By[gp][tc]])
            mk = S.marks()
            outproj(3)
            S.fence(mk)

            cqn = av(MS, 9216, BF16).rearrange("p (c t) -> p c t", t=T)
            ckvn = av(MS + 9216, 4608, BF16)
            krT = av(MS + 13824, 4608, BF16)
            Vc = av(MS + 18432, 6912, BF16).rearrange("p (t c) -> p t c", c=192)
            cosC = av(MS + 25344, 2048, F32)
            sinC = av(MS + 27392, 2048, F32)
            Bcq = [Buf(f"cq{tc}") for tc in range(5)]
            Bckv = [Buf(f"ckv{tc}") for tc in range(5)]
            Bkr = [Buf(f"kr{tc}") for tc in range(5)]
            Brc_ = Buf("ropeC")
            wC1, wbC1 = load_w(win_d[l, :, 1664:1984], 8, 320)
            wC2, wbC2 = load_w(win_d[l, :, 1984:2240], 8, 256)
            wuq, wbuq = load_w(wuq_d[l, :, :], 2, 768)
            wukv_, wbukv = load_w(wukv_d[l, :, :], 1, 512)
            wukv = wukv_[:, 0, :]
            R96 = slice(64, 96)
            epsc = prm[:, L * PL + 1:L * PL + 2]
            def upproj(tc):
                t0, n = TCS[tc]
                lat = tc < 4
                for h in range(4):
                    MEMSET("dve", hT[64:128, h, t0:t0 + n], 0.0, [Bh[h][tc]])
                    pk, pkb = nextps()
                    MM(pk[0:64, 0:n], wukv[:, h * 64:(h + 1) * 64], ckvn[:, t0:t0 + n], True, True, [wbukv, Bckv[tc]], [pkb])
                    ACT(hT[0:64, h, t0:t0 + n], pk[0:64, 0:n], AF.Copy, [pkb], [Bh[h][tc]])
                    ACT(hT[R96, h, t0:t0 + n], krT[R96, t0:t0 + n], AF.Copy, [Bkr[tc]], [Bh[h][tc]])
                    pq, pqb = nextps()
                    MM(pq[0:96, 0:n], wuq[:, 0, h * 96:(h + 1) * 96], cqn[:, 0, t0:t0 + n], True, False, [wbuq, Bcq[tc]], [pqb])
                    MM(pq[0:96, 0:n], wuq[:, 1, h * 96:(h + 1) * 96], cqn[:, 1, t0:t0 + n], False, True, [wbuq, Bcq[tc]], [pqb])
                    ACT(hT[0:64, 4 + h, t0:t0 + n], pq[0:64, 0:n], AF.Copy, [pqb], [Bh[4 + h][tc]])
                    if lat:
                        pqs, pqsb = nextps()
                        MM(pqs[0:96, 0:n], wuq[:, 0, 384 + h * 96:384 + (h + 1) * 96], cqn[:, 0, t0:t0 + n], True, False,
                           [wbuq, Bcq[tc]], [pqsb])
                        MM(pqs[0:96, 0:n], wuq[:, 1, 384 + h * 96:384 + (h + 1) * 96], cqn[:, 1, t0:t0 + n], False, True,
                           [wbuq, Bcq[tc]], [pqsb])
                        TT("dve", tmpf[0][R96, 0:n], pq[R96, 0:n], cosC[R96, 0:n], ALU.mult, [pqb, Brc_], [Btmp[0]])
                        TT("dve", tmpf[1][R96, 0:n], pqs[R96, 0:n], sinC[R96, 0:n], ALU.mult, [pqsb, Brc_], [Btmp[1]])
                        TT("dve", hT[R96, 4 + h, t0:t0 + n], tmpf[0][R96, 0:n], tmpf[1][R96, 0:n], ALU.add,
                           [Btmp[0], Btmp[1]], [Bh[4 + h][tc]])
                    else:
                        ACT(hT[R96, 4 + h, t0:t0 + n], pq[R96, 0:n], AF.Copy, [pqb], [Bh[4 + h][tc]])

            for tc, (t0, n) in enumerate(TCS):
                lat = tc < 4
                pkv, pkvb = inproj_fm(wC1, wbC1, 0, 128, tc)
                pkr, pkrb = inproj_fm(wC1, wbC1, 128, 96, tc)
                if lat:
                    pks, pksb = inproj_fm(wC1, wbC1, 224, 96, tc)
                    load_rope(2, tc, cosC, sinC, Brc_)
                sq, sqb = nextpt()
                ACT(sq[:, 0:n], pkv[:, 0:n], AF.Square, [pkvb], [sqb])
                s1, s1b = nextps()
                MM(s1[:, 0:n], onesb, sq[:, 0:n], True, True, [Bconst, sqb], [s1b])
                ACT(rcs[:, 0:n], s1[:, 0:n], AF.Sqrt, [s1b, Bprm], [Brc], scale=1.0 / 128.0, bias=epsc)
                RECIP(rcs[:, 0:n], rcs[:, 0:n], [Brc], [Brc])
                STT(ckvn[:, t0:t0 + n], pkv[:, 0:n], prm[:, pb + 86:pb + 87], rcs[:, 0:n], ALU.mult, ALU.mult,
                    [pkvb, Bprm, Brc], [Bckv[tc]])
                if lat:
                    TT("dve", tmpf[0][R96, 0:n], pkr[R96, 0:n], cosC[R96, 0:n], ALU.mult, [pkrb, Brc_], [Btmp[0]])
                    TT("dve", tmpf[1][R96, 0:n], pks[R96, 0:n], sinC[R96, 0:n], ALU.mult, [pksb, Brc_], [Btmp[1]])
                    TT("pool", krT[R96, t0:t0 + n], tmpf[0][R96, 0:n], tmpf[1][R96, 0:n], ALU.add, [Btmp[0], Btmp[1]], [Bkr[tc]])
                else:
                    ACT(krT[R96, t0:t0 + n], pkr[R96, 0:n], AF.Copy, [pkrb], [Bkr[tc]])
                pq0, pq0b = inproj_fm(wC2, wbC2, 0, 128, tc)
                pq1, pq1b = inproj_fm(wC2, wbC2, 128, 128, tc)
                sq0, sq0b = nextpt()
                sq1, sq1b = nextpt()
                ACT(sq0[:, 0:n], pq0[:, 0:n], AF.Square, [pq0b], [sq0b])
                ACT(sq1[:, 0:n], pq1[:, 0:n], AF.Square, [pq1b], [sq1b])
                s2, s2b = nextps()
                MM(s2[:, 0:n], onesb, sq0[:, 0:n], True, False, [Bconst, sq0b], [s2b])
                MM(s2[:, 0:n], onesb, sq1[:, 0:n], False, True, [Bconst, sq1b], [s2b])
                ACT(rcs[:, 0:n], s2[:, 0:n], AF.Sqrt, [s2b, Bprm], [Brc], scale=1.0 / 256.0, bias=epsc)
                RECIP(rcs[:, 0:n], rcs[:, 0:n], [Brc], [Brc])
                STT(cqn[:, 0, t0:t0 + n], pq0[:, 0:n], prm[:, pb + 84:pb + 85], rcs[:, 0:n], ALU.mult, ALU.mult,
                    [pq0b, Bprm, Brc], [Bcq[tc]])
                STT(cqn[:, 1, t0:t0 + n], pq1[:, 0:n], prm[:, pb + 85:pb + 86], rcs[:, 0:n], ALU.mult, ALU.mult,
                    [pq1b, Bprm, Brc], [Bcq[tc]])
                upproj(tc)
            for p in range(2):
                Bvc = [Buf(f"vc{tc}") for tc in range(5)]
                MEMSET("pool", Vc[:, :, 64:128], 1.0, Bvc)
                for tc, (t0, n) in enumerate(TCS):
                    nt = n // 128
                    p_, pb_ = nextps()
                    for i in range(nt):
                        t = t0 // 128 + i
                        MM(p_[:, i * 128:(i + 1) * 128], ckvn[:, t * 128:(t + 1) * 128], wukv[:, 256 + p * 128:256 + (p + 1) * 128],
                           True, True, [wbukv, Bckv[tc]], [pb_])
                    pv = p_[:, 0:nt * 128].rearrange("p (a b) -> p a b", b=128)
                    tt = t0 // 128
                    ACT(Vc[:, tt:tt + nt, 0:64], pv[:, :, 0:64], AF.Copy, [pb_], [Bvc[tc]])
                    COPY("dve", Vc[:, tt:tt + nt, 128:192], pv[:, :, 64:128], [pb_], [Bvc[tc]])
                for hh in range(2):
                    h = 2 * p + hh
                    rows = slice(hh * 64, hh * 64 + 64)
                    for qtc, (q0, n) in enumerate(TCS):
                        O, Ob = accps()
                        kbl = list(range(NT)) if qtc < 4 else [16, 17]

                        def score(kb):
                            sp_, spb = nextps()
                            MM(sp_[:, 0:n], hT[:, h, kb * 128:(kb + 1) * 128], hT[:, 4 + h, q0:q0 + n], True, True,
                               [Bh[h][min(kb // 4, 4)], Bh[4 + h][qtc]], [spb])
                            return sp_, spb

                        pend = [score(kbl[0])]
                        if len(kbl) > 1:
                            pend.append(score(kbl[1]))
                        for ki, kb in enumerate(kbl):
                            sp_, spb = pend.pop(0)
                            if ki + 2 < len(kbl):
                                pend.append(score(kbl[ki + 2]))
                            P, Pb = nextpt()
                            ACT(P[:, 0:n], sp_[:, 0:n], AF.Exp, [spb], [Pb], scale=SC_C)
                            MM(O[:, 0:n], Vc[:, kb, hh * 64:hh * 64 + 128], P[:, 0:n], ki == 0, ki == len(kbl) - 1,
                               [Bvc[min(kb // 4, 4)], Pb], [Ob])
                        normalize(O, Ob, hh, yT[rows, p, q0:q0 + n], [By[p][qtc]], n)
            outproj(2)
            S.barrier()


            zb = [av(36864 + i * 1024, 1024, BF16) for i in range(2)]
            zq = [av(38912 + i * 1024, 1024, BF16) for i in range(2)]
            mean_ts = [av(40960 + i * 2048, 2048, F32) for i in range(2)]
            rstd_ts = [av(45056 + i * 2048, 2048, F32) for i in range(2)]
            lt = [av(49152 + i * 2048, 2048, F32) for i in range(2)]
            Bzb = [Buf("zb0"), Buf("zb1")]
            Bzq = [Buf("zq0"), Buf("zq1")]
            Bmeans = [Buf("mean0"), Buf("mean1")]
            Brstds = [Buf("rstd0"), Buf("rstd1")]
            Blt = [Buf("lt0"), Buf("lt1")]
            eps_ln = prm[:, L * PL + 0:L * PL + 1]

            def layernorm(gcol, bcol, emit_h, hmod=None, hBmod=None, hg=160, hb=176):
                hmod = modv if hmod is None else hmod
                hBmod = Bmod if hBmod is None else hBmod
                accs = {}

                def stats_a(tc):
                    t0, n = TCS[tc]
                    sps, spsb = accps()
                    qps, qpsb = accps()
                    accs[tc] = (sps, spsb, qps, qpsb)
                    for c in range(8):
                        z = xT[:, c, t0:t0 + n]
                        zhi = z.bitcast(BF16)[:, 1::2]
                        ACT(zq[c % 2][:, 0:n], z, AF.Square, [Bx[c][tc]], [Bzq[c % 2]])
                        MM(sps[:, 0:n], onesb, zhi, c == 0, c == 7, [Bconst, Bx[c][tc]], [spsb])
                        MM(qps[:, 0:n], onesb, zq[c % 2][:, 0:n], c == 0, c == 7, [Bconst, Bzq[c % 2]], [qpsb])

                def stats_b(tc):
                    t0, n = TCS[tc]
                    sps, spsb, qps, qpsb = accs[tc]
                    mean_t, Bmean = mean_ts[tc % 2], Bmeans[tc % 2]
                    rstd_t, Brstd = rstd_ts[tc % 2], Brstds[tc % 2]
                    ACT(mean_t[:, 0:n], sps[:, 0:n], AF.Identity, [spsb], [Bmean], scale=1.0 / D)
                    TT("dve", rstd_t[:, 0:n], mean_t[:, 0:n], mean_t[:, 0:n], ALU.mult, [Bmean], [Brstd])
                    STT(rstd_t[:, 0:n], qps[:, 0:n], 1.0 / D, rstd_t[:, 0:n], ALU.mult, ALU.subtract, [qpsb, Brstd], [Brstd])
                    ACT(rstd_t[:, 0:n], rstd_t[:, 0:n], AF.Sqrt, [Brstd, Bprm], [Brstd], scale=1.0, bias=eps_ln)
                    RECIP(rstd_t[:, 0:n], rstd_t[:, 0:n], [Brstd], [Brstd])

                def normz(tc):
                    t0, n = TCS[tc]
                    s = 1 if tc == 4 else 0
                    mean_t, Bmean = mean_ts[tc % 2], Bmeans[tc % 2]
                    rstd_t, Brstd = rstd_ts[tc % 2], Brstds[tc % 2]
                    for c in range(8):
                        z = xT[:, c, t0:t0 + n]
                        tq = lt[c % 2]
                        TT("dve", tq[:, 0:n], z, mean_t[:, 0:n], ALU.subtract, [Bx[c][tc], Bmean], [Blt[c % 2]])
                        TT("dve", tq[:, 0:n], tq[:, 0:n], rstd_t[:, 0:n], ALU.mult, [Blt[c % 2], Brstd], [Blt[c % 2]])
                        ACT(z, tq[:, 0:n], AF.Identity, [Blt[c % 2], Bprm], [Bx[c][tc]],
                            scale=prm[:, pb + gcol + c:pb + gcol + c + 1], bias=prm[:, pb + bcol + c:pb + bcol + c + 1])
                        if emit_h:
                            ACT(hT[:, c, t0:t0 + n], tq[:, 0:n], AF.Identity, [Blt[c % 2], hBmod], [Bh[c][tc]],
                                scale=hmod[:, col(hg, c, s):col(hg, c, s) + 1], bias=hmod[:, col(hb, c, s):col(hb, c, s) + 1])

                stats_a(0)
                stats_b(0)
                for tc in range(5):
                    if tc + 1 < 5:
                        stats_a(tc + 1)
                    normz(tc)
                    if tc + 1 < 5:
                        stats_b(tc + 1)

            layernorm(48, 56, True)

            hid = [av(i * 18432, 18432, BF16).rearrange("p (j t) -> p j t", t=T) for i in range(2)]
            sgf = lt
            Bsg = Blt
            pieces = [(0, 4), (4, 4), (8, 4), (12, 4), (16, 4), (20, 2)]
            nxt = l + 1 < nl
            if nxt:
                nmps, nmpb = accps()
            for pi, (j0, nj) in enumerate(pieces):
                hd = hid[pi % 2]
                Bhd = [[Buf(f"hid{j}_{tc}") for tc in range(5)] for j in range(nj)]
                halves = []
                for half in range((nj + 1) // 2):
                    jj0 = half * 2
                    njj = min(2, nj - jj0)
                    halves.append((jj0, njj) + load_w(wgu_d[l, :, (j0 + jj0) * 256:(j0 + jj0 + njj) * 256], 8, njj * 256))
                wdv, wdb = load_w(wdn_d[l, j0 * 128:(j0 + nj) * 128, :], nj, 1024)
                for (jj0, njj, wg, wgb) in halves:
                    for jj in range(njj):
                        j = jj0 + jj
                        for tc, (t0, n) in enumerate(TCS):
                            pg, pgb = nextps()
                            pu, pub = nextps()
                            for k in range(8):
                                MM(pg[:, 0:n], wg[:, k, jj * 256:jj * 256 + 128], hT[:, k, t0:t0 + n], k == 0, k == 7,
                                   [wgb, Bh[k][tc]], [pgb])
                            for k in range(8):
                                MM(pu[:, 0:n], wg[:, k, jj * 256 + 128:jj * 256 + 256], hT[:, k, t0:t0 + n], k == 0, k == 7,
                                   [wgb, Bh[k][tc]], [pub])
                            si = (j + tc) % 2
                            ACT(sgf[si][:, 0:n], pg[:, 0:n], AF.Silu, [pgb], [Bsg[si]])
                            TT("dve", hd[:, j, t0:t0 + n], sgf[si][:, 0:n], pu[:, 0:n], ALU.mult, [Bsg[si], pub], [Bhd[j][tc]])
                for f in range(8):
                    for tc, (t0, n) in enumerate(TCS):
                        s = 1 if tc == 4 else 0
                        p_, pb_ = nextps()
                        for j in range(nj):
                            MM(p_[:, 0:n], wdv[:, j, f * 128:(f + 1) * 128], hd[:, j, t0:t0 + n], j == 0, j == nj - 1,
                               [wdb, Bhd[j][tc]], [pb_])
                        STT(xT[:, f, t0:t0 + n], p_[:, 0:n], modv[:, col(144, f, s):col(144, f, s) + 1], xT[:, f, t0:t0 + n],
                            ALU.mult, ALU.add, [pb_, Bmod, Bx[f][tc]], [Bx[f][tc]])
                if nxt:
                    mod_piece(l + 1, 2 * pi, nmps, nmpb)
                    mod_piece(l + 1, 2 * pi + 1, nmps, nmpb)
            if nxt:
                mod_finish(l + 1, nmps, nmpb)
            if nxt:
                mk_ffn = S.marks()
                nmodv = modv_all[:, ((l + 1) % 2) * 192:((l + 1) % 2) * 192 + 192]
                layernorm(64, 72, True, nmodv, Bmods[(l + 1) % 2], 16, 64)
                mk_ln = S.marks()
                S.fence(mk_ffn)
                return mk_ln
            layernorm(64, 72, False)
            S.barrier()
            return None

        mps0, mpb0 = accps()
        load_x(mps0, mpb0)
        S.barrier()
        mod_finish(0, mps0, mpb0)
        lnm = None
        for l in range(nl):
            lnm = layer(l, lnm)

        osb = [av(0, 4096, F32), av(4096, 4096, F32)]
        Bos = [Buf("os0"), Buf("os1")]
        Bout = Buf("out")
        for t in range(16):
            tc = t // 4
            o_, ob_ = osb[t % 2], Bos[t % 2]
            for cg in range(2):
                p_, pb_ = nextps()
                for i in range(4):
                    c = cg * 4 + i
                    S.add("pe", (lambda oo, ii: (lambda e: e.transpose(out=oo, in_=ii, identity=ident[:, :])))(
                        p_[:, i * 128:(i + 1) * 128], xT[:, c, t * 128:(t + 1) * 128]),
                        reads=[Bx[c][tc], Bident], writes=[pb_])
                if cg == 0:
                    ACT(o_[:, 0:512], p_[:, 0:512], AF.Copy, [pb_], [ob_])
                else:
                    COPY("dve", o_[:, 512:1024], p_[:, 0:512], [pb_], [ob_])
            DMA("sp", out_d[t * 128:(t + 1) * 128, :], o_, [ob_], [Bout, ob_])
        S.add("sp", lambda e: e.nop(), reads=[Bout])
        S.emit(st)
    return nc


def _partner(n, half):
    j = np.arange(n)
    q = half // 2
    return np.where((j % half) < q, j + q, j - q)


def _rope_tables():
    t = np.arange(S_LAT)
    rows = (t // 64).astype(np.float32)
    cols = (t % 64).astype(np.float32)
    tabs = np.zeros((4, 128, S_LAT), np.float32)
    tabs[0] = 1.0
    tabs[2] = 1.0
    inv = (10000.0 ** (-np.arange(0, 32, 2, dtype=np.float32) / 32)).astype(np.float32)
    for p in range(128):
        j = p % 64
        pos = rows if j < 32 else cols
        jj = j % 32
        i = jj % 16
        ang = pos * inv[i]
        sign = -1.0 if jj < 16 else 1.0
        tabs[0, p] = np.cos(ang)
        tabs[1, p] = sign * np.sin(ang)
    inv = (10000.0 ** (-np.arange(0, 16, 2, dtype=np.float32) / 16)).astype(np.float32)
    for p in range(64, 96):
        j = p - 64
        pos = rows if j < 16 else cols
        jj = j % 16
        i = jj % 8
        ang = pos * inv[i]
        sign = -1.0 if jj < 8 else 1.0
        tabs[2, p] = np.cos(ang)
        tabs[3, p] = sign * np.sin(ang)
    return tabs


def _rpb_index():
    idx = np.full((21, 128, 128), 465, np.int64)
    ql = np.arange(128)
    kl = np.arange(128)
    for j in [2, 0, 1, 14, 15]:
        slot0, kbs = b_class(j)
        r = 2 * j + ql // 64
        qc = ql % 64
        r_start = np.clip(r - 4, 0, 24)
        c_start = np.clip(qc - 8, 0, 48)
        for si, kb in enumerate(kbs):
            kr = 2 * kb + kl // 64
            kc = kl % 64
            vr = (kr[:, None] >= r_start[None, :]) & (kr[:, None] < r_start[None, :] + 8)
            vc = (kc[:, None] >= c_start[None, :]) & (kc[:, None] < c_start[None, :] + 16)
            dr = kr[:, None] - r[None, :] + 7
            dc = np.clip(kc[:, None] - qc[None, :] + 15, 0, 30)
            flat = np.clip(dr, 0, 14) * 31 + dc
            idx[slot0 + si] = np.where(vr & vc, flat, 465)
    return idx


_CONST = {}


def _consts():
    if not _CONST:
        _CONST["rope"] = _rope_tables()
        _CONST["rpbidx"] = _rpb_index()
        cst = np.zeros((128, 512), np.float32)
        cst[:, 0:128] = np.eye(128, dtype=np.float32)
        k = np.arange(128)[:, None]
        q = np.arange(128)[None, :]
        cst[:, 128:256] = np.where(k >= q, 0.0, NEG)
        cst[:, 256:384] = np.where(k <= q, 0.0, NEG)
        cst[:, 384:512] = np.eye(128, dtype=np.float32)
        _CONST["cst"] = cst
    return _CONST


def _prep_shared(w_mod, b_mod, w_in, a_sink, b_rpb, c_q_norm, c_kv_norm, c_w_uq, c_w_ukv, d_ln_g, d_ln_b, d_ws, d_bs,
                 w_out, ln1_g, ln1_b, w_gu, w_down, ln2_g, ln2_b):
    C = _consts()
    f = np.float32
    ak0, av0, bk0, bv0, ckv0, ckr0, aq0, bq0, ccq0, du0, dv0 = 0, 128, 256, 512, 768, 896, 928, 1184, 1440, 1696, 1952
    p64 = _partner(64, 32)
    p32 = _partner(32, 16)
    cols = []
    ak = ak0 + np.arange(128)
    ak_sw = ak0 + np.concatenate([p64, 64 + p64])
    aqh = lambda h: aq0 + h * 64 + np.arange(64)
    aqh_sw = lambda h: aq0 + h * 64 + p64
    cols += [ak, ak_sw, np.concatenate([aqh(0), aqh(2)]), np.concatenate([aqh_sw(0), aqh_sw(2)])]
    cols += [np.concatenate([aqh(1), aqh(3)]), np.concatenate([aqh_sw(1), aqh_sw(3)]), av0 + np.arange(128)]
    for p in range(2):
        cols += [bk0 + p * 128 + np.arange(128), bq0 + p * 128 + np.arange(128), bv0 + p * 128 + np.arange(128)]
    cols += [ckv0 + np.arange(128), np.concatenate([ckv0 + 64 + np.arange(64), ckr0 + np.arange(32)]),
             np.concatenate([ckv0 + 64 + np.arange(64), ckr0 + p32])]
    cols += [ccq0 + np.arange(256), du0 + np.arange(256), dv0 + np.arange(256)]
    cols = np.concatenate(cols)
    assert cols.shape[0] == NIN
    w_in_p = np.ascontiguousarray(np.asarray(w_in, f)[:, :, cols])
    uq_sw = np.concatenate([np.concatenate([h * 96 + np.arange(64), h * 96 + 64 + p32]) for h in range(4)])
    w_uq_p = np.ascontiguousarray(np.concatenate([np.asarray(c_w_uq, f), np.asarray(c_w_uq, f)[:, :, uq_sw]], axis=2))
    ukv_cols = np.concatenate([np.concatenate([h * 128 + np.arange(64) for h in range(4)]),
                               np.concatenate([h * 128 + 64 + np.arange(64) for h in range(4)])])
    w_ukv_p = np.ascontiguousarray(np.asarray(c_w_ukv, f)[:, :, ukv_cols])
    wsT = np.ascontiguousarray(np.transpose(np.asarray(d_ws, f), (0, 3, 1, 2)).reshape(L, 128, 512))
    rows = np.arange(1024)
    arow = np.concatenate([h * 64 + np.arange(64) for h in (0, 2, 1, 3)])
    rows[:256] = arow
    w_out_p = np.ascontiguousarray(np.asarray(w_out, f)[:, rows, :])
    gu_cols = np.concatenate([np.concatenate([j * 128 + np.arange(128), FF + j * 128 + np.arange(128)]) for j in range(22)])
    w_gu_p = np.ascontiguousarray(np.asarray(w_gu, f)[:, :, gu_cols])
    prm = np.zeros((128, NPRM), f)
    fm = lambda v, n: np.asarray(v, f).reshape(n, 128).T
    for l in range(L):
        b = l * PL
        prm[:, b:b + 48] = fm(b_mod[l], 48)
        prm[:, b + 48:b + 56] = fm(ln1_g[l], 8)
        prm[:, b + 56:b + 64] = fm(ln1_b[l], 8)
        prm[:, b + 64:b + 72] = fm(ln2_g[l], 8)
        prm[:, b + 72:b + 80] = fm(ln2_b[l], 8)
        prm[:, b + 80:b + 84] = np.broadcast_to(np.asarray(a_sink[l], f)[None, :], (128, 4))
        prm[:, b + 84:b + 86] = fm(c_q_norm[l], 2)
        prm[:, b + 86:b + 87] = fm(c_kv_norm[l], 1)
    prm[:, L * PL + 0] = EPS / (ALPHA * ALPHA)
    prm[:, L * PL + 1] = EPS
    dvec1 = np.concatenate([np.asarray(d_ln_g, f), np.asarray(d_ln_b, f), np.asarray(d_bs, f).reshape(L, 512)], axis=1)
    dvec = np.ascontiguousarray(np.broadcast_to(dvec1[:, None, :], (L, 128, 1024)))
    rp = np.asarray(b_rpb, f).reshape(L, 4, 465)
    ext = np.concatenate([rp, np.full((L, 4, 1), NEG, f)], axis=2)
    tab = ext[:, :, C["rpbidx"]]
    rpbt = np.ascontiguousarray(np.transpose(tab, (0, 1, 3, 2, 4)).reshape(L, 4, 128, 21 * 128))
    return {"w_mod": np.ascontiguousarray(np.asarray(w_mod, f)), "w_in": w_in_p, "w_uq": w_uq_p, "w_ukv": w_ukv_p,
            "wsT": wsT, "w_out": w_out_p, "w_gu": w_gu_p, "w_down": np.ascontiguousarray(np.asarray(w_down, f)),
            "prm": prm, "dvec": dvec, "rpbt": rpbt, "rope": C["rope"], "cst": C["cst"]}


_NC_CACHE = {}


def kernel(x, c, ctx, c_ctx, w_mod, b_mod, w_in, a_sink, b_rpb, c_q_norm, c_kv_norm, c_w_uq, c_w_ukv,
           d_ln_g, d_ln_b, d_ws, d_bs, w_out, ln1_g, ln1_b, w_gu, w_down, ln2_g, ln2_b, _nl=L, _cores=8):
    shared = _prep_shared(w_mod, b_mod, w_in, a_sink, b_rpb, c_q_norm, c_kv_norm, c_w_uq, c_w_ukv, d_ln_g, d_ln_b,
                          d_ws, d_bs, w_out, ln1_g, ln1_b, w_gu, w_down, ln2_g, ln2_b)
    x = np.asarray(x, np.float32)
    ctx = np.asarray(ctx, np.float32)
    c = np.asarray(c, np.float32)
    c_ctx = np.asarray(c_ctx, np.float32)
    in_maps = []
    for b in range(_cores):
        m = dict(shared)
        m["x"] = np.ascontiguousarray(np.concatenate([x[b], ctx[b]], axis=0))
        cT = np.zeros((128, 16), np.float32)
        cT[:, 0::2] = c[b].reshape(8, 128).T
        cT[:, 1::2] = c_ctx.reshape(8, 128).T
        m["cT"] = cT
        in_maps.append(m)
    if _nl not in _NC_CACHE:
        _NC_CACHE[_nl] = build(_nl)
    res = run_bass_kernel_spmd(_NC_CACHE[_nl], in_maps, core_ids=list(range(_cores)))
    return np.stack([np.asarray(r["out"], np.float32) for r in res.results], axis=0)
```
